# Optimizing a Trainium2 kernel written in Bass

```python
import jax, jax.numpy as jnp
from jax import lax
import numpy as np

D_MODEL = 1024
BATCH = 8
SEQ = 4096
DEPTH = 4

POOL_GROUPS = 4
POOL_GROUP_DIM = D_MODEL // 16
POOL_WIDTH = POOL_GROUPS * POOL_GROUP_DIM
POOL_WINDOWS = (2, 4, 8, 16)

MLA_HEADS = 8
MLA_NOPE_DIM = D_MODEL // 16
MLA_ROPE_DIM = D_MODEL // 32
MLA_V_DIM = D_MODEL // 16
MLA_Q_RANK = 3 * D_MODEL // 8
MLA_KV_RANK = D_MODEL // 4
MLA_QK_DIM = MLA_NOPE_DIM + MLA_ROPE_DIM
MLA_WIDTH = MLA_HEADS * MLA_V_DIM
ROPE_THETA = 10000.0
Q_BLOCK = 128

LRU_WIDTH = D_MODEL // 4
LRU_HEADS = 4
LRU_HEAD_DIM = LRU_WIDTH // LRU_HEADS
CONV_WIDTH = 4
LRU_C = 8.0

N_BRANCH = 3
D_FF = 4 * D_MODEL
EPS = 1e-6

IN_POOL_END = POOL_WIDTH
IN_Q_END = IN_POOL_END + MLA_Q_RANK
IN_KV_END = IN_Q_END + MLA_KV_RANK + MLA_ROPE_DIM
IN_LRU_END = IN_KV_END + LRU_WIDTH
IN_COLS = IN_LRU_END + N_BRANCH * D_MODEL

kernel_name = "hybrid_pool_mla_rglru_gated"


def rms_norm(x, g):
    xf = x.astype(jnp.float32)
    y = xf * lax.rsqrt(jnp.mean(xf * xf, axis=-1, keepdims=True) + EPS)
    return (y * g.astype(jnp.float32)).astype(x.dtype)


def rope_tables(seq):
    pos = jnp.arange(seq, dtype=jnp.float32)
    inv = ROPE_THETA ** (-jnp.arange(0, MLA_ROPE_DIM, 2, dtype=jnp.float32) / MLA_ROPE_DIM)
    ang = pos[:, None] * inv[None, :]
    return jnp.cos(ang), jnp.sin(ang)


def apply_rope(x, cos, sin):
    x1, x2 = jnp.split(x.astype(jnp.float32), 2, axis=-1)
    shape = (cos.shape[0],) + (1,) * (x.ndim - 3) + (cos.shape[1],)
    c, s = cos.reshape(shape), sin.reshape(shape)
    return jnp.concatenate([x1 * c - x2 * s, x2 * c + x1 * s], axis=-1).astype(x.dtype)


def pool_mixer(u, w_grp, scale):
    B, S, _ = u.shape
    ug = u.reshape(B, S, POOL_GROUPS, POOL_GROUP_DIM)
    cs0 = jnp.pad(jnp.cumsum(ug.astype(jnp.float32), axis=1), ((0, 0), (1, 0), (0, 0), (0, 0)))
    t = jnp.arange(S, dtype=jnp.int32)[:, None]
    win = jnp.array(POOL_WINDOWS, dtype=jnp.int32)[None, :]
    start = jnp.maximum(t + 1 - win, 0)
    lagged = cs0[:, start, jnp.arange(POOL_GROUPS)[None, :], :]
    count = jnp.minimum(t + 1, win).astype(jnp.float32)[None, :, :, None]
    mixed = ((cs0[:, 1:] - lagged) / count - ug.astype(jnp.float32)).astype(u.dtype)
    y = jnp.einsum('bsgc,gcd->bsgd', mixed, w_grp)
    return y.reshape(B, S, POOL_WIDTH) * scale


def causal_attention(q, k, v):
    B, S, H, Dk = q.shape
    nb = S // Q_BLOCK
    scale = Dk ** -0.5
    qb = q.reshape(B, nb, Q_BLOCK, H, Dk).transpose(1, 0, 2, 3, 4)
    k_pos = jnp.arange(S, dtype=jnp.int32)

    def block(args):
        q_blk, i = args
        s = jnp.einsum('bqhd,bkhd->bhqk', q_blk, k, preferred_element_type=jnp.float32) * scale
        q_pos = i * Q_BLOCK + jnp.arange(Q_BLOCK, dtype=jnp.int32)
        s = jnp.where(k_pos[None, :] <= q_pos[:, None], s, -jnp.inf)
        p = jax.nn.softmax(s, axis=-1).astype(v.dtype)
        return jnp.einsum('bhqk,bkhd->bqhd', p, v)

    o = lax.map(block, (qb, jnp.arange(nb, dtype=jnp.int32)))
    return o.transpose(1, 0, 2, 3, 4).reshape(B, S, H, v.shape[-1])


def mla_mixer(q_lat, kv_lat, g_q, w_q_up, g_kv, w_kv_up, cos, sin):
    B, S, _ = q_lat.shape
    q = (rms_norm(q_lat, g_q) @ w_q_up).reshape(B, S, MLA_HEADS, MLA_QK_DIM)
    q_nope, q_rope = q[..., :MLA_NOPE_DIM], q[..., MLA_NOPE_DIM:]
    c_kv, k_rope = kv_lat[..., :MLA_KV_RANK], kv_lat[..., MLA_KV_RANK:]
    kv = (rms_norm(c_kv, g_kv) @ w_kv_up).reshape(B, S, MLA_HEADS, MLA_NOPE_DIM + MLA_V_DIM)
    k_nope, v = kv[..., :MLA_NOPE_DIM], kv[..., MLA_NOPE_DIM:]
    q_rope = apply_rope(q_rope, cos, sin)
    k_rope = apply_rope(k_rope, cos, sin)
    q = jnp.concatenate([q_nope, q_rope], axis=-1)
    k = jnp.concatenate([k_nope, jnp.broadcast_to(k_rope[:, :, None, :], (B, S, MLA_HEADS, MLA_ROPE_DIM))], axis=-1)
    o = causal_attention(q, k, v)
    return o.reshape(B, S, MLA_WIDTH)


def rg_lru_mixer(u, conv_w, conv_b, w_a, b_a, w_x, b_x, lam):
    B, S, C = u.shape
    u = lax.conv_general_dilated(u, conv_w[:, None, :], window_strides=(1,),
                                 padding=[(CONV_WIDTH - 1, 0)],
                                 dimension_numbers=('NWC', 'WIO', 'NWC'),
                                 feature_group_count=C) + conv_b
    ub = u.reshape(B, S, LRU_HEADS, LRU_HEAD_DIM)
    r = jax.nn.sigmoid(jnp.einsum('bshc,hcd->bshd', ub, w_a).reshape(B, S, C) + b_a)
    i = jax.nn.sigmoid(jnp.einsum('bshc,hcd->bshd', ub, w_x).reshape(B, S, C) + b_x)
    log_a = -LRU_C * r.astype(jnp.float32) * jax.nn.softplus(-lam.astype(jnp.float32))
    a = jnp.exp(log_a)
    b = jnp.sqrt(-jnp.expm1(2.0 * log_a)) * (i * u).astype(jnp.float32)

    def combine(left, right):
        a_l, b_l = left
        a_r, b_r = right
        return a_l * a_r, a_r * b_l + b_r

    _, h = lax.associative_scan(combine, (a, b), axis=1)
    return h.astype(u.dtype)


def setup_inputs(seed: int = 0) -> dict:
    key = jax.random.key(seed)
    ks = jax.random.split(key, 26)
    f32 = jnp.float32

    def nrm(k, shape, fan_in):
        return jax.random.normal(k, shape, f32) * (fan_in ** -0.5)

    def gain(k, shape):
        return 1.0 + 0.05 * jax.random.normal(k, shape, f32)

    def small(k, shape):
        return 0.01 * jax.random.normal(k, shape, f32)

    a8 = jax.random.uniform(ks[17], (DEPTH, LRU_WIDTH), f32, minval=0.9, maxval=0.999)
    s = a8 ** (1.0 / LRU_C)
    lru_lambda = jnp.log(s) - jnp.log1p(-s)

    return {
        "x": jax.random.normal(ks[0], (BATCH, SEQ, D_MODEL), f32),
        "g_mix": gain(ks[1], (DEPTH, D_MODEL)),
        "w_in": nrm(ks[2], (DEPTH, D_MODEL, IN_COLS), D_MODEL),
        "w_pool_grp": nrm(ks[3], (DEPTH, POOL_GROUPS, POOL_GROUP_DIM, POOL_GROUP_DIM), POOL_GROUP_DIM),
        "pool_scale": 1.0 + 0.1 * jax.random.normal(ks[4], (DEPTH, POOL_WIDTH), f32),
        "w_pool_proj": nrm(ks[5], (DEPTH, POOL_WIDTH, D_MODEL), POOL_WIDTH),
        "g_q": gain(ks[6], (DEPTH, MLA_Q_RANK)),
        "w_q_up": nrm(ks[7], (DEPTH, MLA_Q_RANK, MLA_HEADS * MLA_QK_DIM), MLA_Q_RANK),
        "g_kv": gain(ks[8], (DEPTH, MLA_KV_RANK)),
        "w_kv_up": nrm(ks[9], (DEPTH, MLA_KV_RANK, MLA_HEADS * (MLA_NOPE_DIM + MLA_V_DIM)), MLA_KV_RANK),
        "w_mla_o": nrm(ks[10], (DEPTH, MLA_WIDTH, D_MODEL), MLA_WIDTH),
        "conv_w": nrm(ks[11], (DEPTH, CONV_WIDTH, LRU_WIDTH), CONV_WIDTH),
        "conv_b": small(ks[12], (DEPTH, LRU_WIDTH)),
        "w_lru_a": nrm(ks[13], (DEPTH, LRU_HEADS, LRU_HEAD_DIM, LRU_HEAD_DIM), LRU_HEAD_DIM),
        "b_lru_a": small(ks[14], (DEPTH, LRU_WIDTH)),
        "w_lru_x": nrm(ks[15], (DEPTH, LRU_HEADS, LRU_HEAD_DIM, LRU_HEAD_DIM), LRU_HEAD_DIM),
        "b_lru_x": small(ks[16], (DEPTH, LRU_WIDTH)),
        "lru_lambda": lru_lambda,
        "w_lru_proj": nrm(ks[18], (DEPTH, LRU_WIDTH, D_MODEL), LRU_WIDTH),
        "b_gate": small(ks[19], (DEPTH, N_BRANCH * D_MODEL)),
        "w_out": nrm(ks[20], (DEPTH, D_MODEL, D_MODEL), D_MODEL),
        "g_ffn": gain(ks[21], (DEPTH, D_MODEL)),
        "w_ff1": nrm(ks[22], (DEPTH, D_MODEL, D_FF), D_MODEL),
        "w_ff2": nrm(ks[23], (DEPTH, D_FF, D_MODEL), D_FF),
        "g_final": gain(ks[24], (D_MODEL,)),
    }


def reference(x, g_mix, w_in, w_pool_grp, pool_scale, w_pool_proj, g_q, w_q_up, g_kv, w_kv_up,
              w_mla_o, conv_w, conv_b, w_lru_a, b_lru_a, w_lru_x, b_lru_x, lru_lambda, w_lru_proj,
              b_gate, w_out, g_ffn, w_ff1, w_ff2, g_final):
    B, S, D = x.shape
    cos, sin = rope_tables(S)
    for l in range(DEPTH):
        h = rms_norm(x, g_mix[l])
        proj = h @ w_in[l]
        u_pool, q_lat, kv_lat, u_lru, gate_logit = jnp.split(
            proj, [IN_POOL_END, IN_Q_END, IN_KV_END, IN_LRU_END], axis=-1)
        y_pool = pool_mixer(u_pool, w_pool_grp[l], pool_scale[l]) @ w_pool_proj[l]
        y_mla = mla_mixer(q_lat, kv_lat, g_q[l], w_q_up[l], g_kv[l], w_kv_up[l], cos, sin) @ w_mla_o[l]
        y_lru = rg_lru_mixer(u_lru, conv_w[l], conv_b[l], w_lru_a[l], b_lru_a[l],
                             w_lru_x[l], b_lru_x[l], lru_lambda[l]) @ w_lru_proj[l]
        gates = jax.nn.sigmoid(gate_logit + b_gate[l]).reshape(B, S, N_BRANCH, D)
        merged = gates[:, :, 0] * y_pool + gates[:, :, 1] * y_mla + gates[:, :, 2] * y_lru
        x = x + merged @ w_out[l]
        h = rms_norm(x, g_ffn[l])
        x = x + jnp.square(jax.nn.relu(h @ w_ff1[l])) @ w_ff2[l]
    return rms_norm(x, g_final)
```

```python
import numpy as np
from contextlib import ExitStack
import concourse.bass as bass
import concourse.mybir as mybir
from concourse.bass_utils import run_bass_kernel_spmd

F32 = mybir.dt.float32
BF16 = mybir.dt.bfloat16
AF = mybir.ActivationFunctionType
ALU = mybir.AluOpType

D = 1024
T = 512
NH = 8
EPS = 1e-6
SEG = {}
_o = 0
for _n, _w in [("wina", 8 * 1216), ("wq", 3 * 1024), ("wk", 2 * 512), ("wv", 2 * 512), ("wbd", 6 * 128),
               ("wing", 8 * 3072), ("womla", 4 * 1024), ("wpp", 2 * 1024), ("wlp", 2 * 1024), ("wout", 8 * 1024),
               ("w1a", 8 * 2048), ("w2a", 16 * 1024), ("w1b", 8 * 2048), ("w2b", 16 * 1024)]:
    SEG[_n] = (_o, _w)
    _o += _w
WTOT = _o
NPL = 63


class Op:
    __slots__ = ("eng", "fn", "idx", "waits", "inc", "semval", "dma", "dsem", "dval", "clk", "dclk")


class _Rec:
    def __getattr__(self, name):
        def f(*a, **k):
            return (name, a, k)
        return f


_REC = _Rec()


def SK(key, sub):
    return ("§", key, sub)


class Sched:
    EPOCH = 30000
    SAME_ENG_WAR = True

    def __init__(self, nc, n_dma_sems=24):
        self.nc = nc
        self.h = {"pe": nc.tensor, "act": nc.scalar, "dve": nc.vector, "pool": nc.gpsimd, "sp": nc.sync}
        self.ops = {k: [] for k in self.h}
        self.lastw = {}
        self.readers = {}
        self.known = {k: {} for k in self.h}
        self.kdma = {k: {} for k in self.h}
        self.nds = n_dma_sems
        self.dcnt = [0] * n_dma_sems
        self.dlast = [None] * n_dma_sems
        self.drr = 0
        self.uid = 0
        self.npre = 0

    @staticmethod
    def _merge(a, b):
        for k, v in b.items():
            if a.get(k, -1) < v:
                a[k] = v

    @staticmethod
    def _ks(k):
        if isinstance(k, tuple) and len(k) == 3 and k[0] == "§":
            return k[1], k[2]
        return k, None

    def op(self, eng, fn, reads=(), writes=(), dma=False, pre=False):
        o = Op()
        o.eng = eng; o.fn = fn(_REC); o.idx = len(self.ops[eng]); o.dma = dma; o.inc = False; o.semval = 0
        o.dsem = None; o.dval = 0
        deps = []
        for k in reads:
            k, sb = self._ks(k)
            lw = self.lastw.get(k)
            if lw:
                if sb is None:
                    for w in lw.values():
                        deps.append((w, 0))
                else:
                    for s_ in (sb, None):
                        w = lw.get(s_)
                        if w is not None:
                            deps.append((w, 0))
        for k in writes:
            k, sb = self._ks(k)
            lw = self.lastw.get(k)
            if lw:
                if sb is None:
                    for w in lw.values():
                        deps.append((w, 1))
                else:
                    for s_ in (sb, None):
                        w = lw.get(s_)
                        if w is not None:
                            deps.append((w, 1))
            rd = self.readers.get(k)
            if rd:
                for s_, dd in rd.items():
                    if sb is None or s_ is None or s_ == sb:
                        for r in dd.values():
                            deps.append((r, 1))
        if dma and pre:
            o.dsem = ("pre", self.npre); o.dval = 16
            self.npre += 1
        elif dma:
            s = self.drr
            self.drr = (s + 1) % self.nds
            if self.dlast[s] is not None:
                deps.append((self.dlast[s], 1))
            self.dcnt[s] += 16
            o.dsem = s; o.dval = self.dcnt[s]; self.dlast[s] = o
        kn = self.known[eng]; kd = self.kdma[eng]
        waits = []
        for d, kind in deps:
            if d is o:
                continue
            if d.dma:
                if kd.get(d.dsem, 0) >= d.dval:
                    continue
                waits.append(d)
                kd[d.dsem] = d.dval
                self._merge(kn, d.clk); self._merge(kd, d.dclk)
            else:
                if d.eng == eng and (eng == "pe" or (kind == 1 and not self.SAME_ENG_WAR)):
                    continue
                if kn.get(d.eng, -1) >= d.idx:
                    continue
                waits.append(d)
                d.inc = True
                kn[d.eng] = d.idx
                self._merge(kn, d.clk); self._merge(kd, d.dclk)
        o.waits = waits
        o.clk = dict(kn); o.dclk = dict(kd)
        self.uid += 1
        for k in reads:
            k, sb = self._ks(k)
            self.readers.setdefault(k, {}).setdefault(sb, {})[("d", self.uid) if dma else eng] = o
        for k in writes:
            k, sb = self._ks(k)
            if sb is None:
                self.lastw[k] = {None: o}
                self.readers[k] = {}
            else:
                self.lastw.setdefault(k, {})[sb] = o
                rd = self.readers.get(k)
                if rd is not None:
                    rd[sb] = {}
        self.ops[eng].append(o)
        return o

    def barrier(self):
        lasts = []
        for e_, lst in self.ops.items():
            if e_ == "sp":
                continue
            for o_ in reversed(lst):
                if o_.fn is not None and not o_.dma:
                    lasts.append(o_)
                    break
        dl = [d for d in self.dlast if d is not None]
        for eng in self.h:
            kn = self.known[eng]; kd = self.kdma[eng]
            waits = []
            for d in lasts:
                if d.eng == eng or kn.get(d.eng, -1) >= d.idx:
                    continue
                waits.append(d); d.inc = True; kn[d.eng] = d.idx
            for d in dl:
                if kd.get(d.dsem, 0) >= d.dval:
                    continue
                waits.append(d); kd[d.dsem] = d.dval
            if waits:
                o = Op()
                o.eng = eng; o.fn = None; o.idx = len(self.ops[eng]); o.dma = False; o.inc = False; o.semval = 0
                o.dsem = None; o.dval = 0; o.waits = waits; o.clk = dict(kn); o.dclk = dict(kd)
                self.ops[eng].append(o)
        self.lastw = {k: v for k, v in self.lastw.items() if isinstance(k, tuple) and k[0] == "wbf"}
        self.readers = {}

    def emit(self, es):
        nc = self.nc
        E = self.EPOCH
        esems = {}
        for e, lst in self.ops.items():
            c = 0
            for o in lst:
                if o.inc and not o.dma and o.fn is not None:
                    c += 1
                    o.semval = c
            esems[e] = [es.enter_context(nc.semaphore(f"s_{e}_{i}")) for i in range(c // E + 1)]
        dsems = {i: es.enter_context(nc.semaphore(f"s_dma_{i}")) for i in range(self.nds)}
        for i in range(self.npre):
            dsems[("pre", i)] = es.enter_context(nc.semaphore(f"s_pre_{i}"))
        fin = es.enter_context(nc.semaphore("s_fin"))
        block = es.enter_context(nc.Block())
        ops = self.ops

        def run(e, name):
            lst = ops[name]
            for o in lst:
                for d in o.waits:
                    if d.dma:
                        e.wait_ge(dsems[d.dsem], d.dval)
                    else:
                        v = d.semval - 1
                        e.wait_ge(esems[d.eng][v // E], v % E + 1)
                if o.fn is None:
                    continue
                ins = getattr(e, o.fn[0])(*o.fn[1], **o.fn[2])
                if o.dma:
                    ins.then_inc(dsems[o.dsem], 16)
                elif o.inc:
                    v = o.semval - 1
                    ins.then_inc(esems[name][v // E], 1)

        @block.sync
        def _(e):
            run(e, "sp")

        @block.tensor
        def _(e):
            run(e, "pe")

        @block.scalar
        def _(e):
            run(e, "act")

        @block.vector
        def _(e):
            run(e, "dve")

        @block.gpsimd
        def _(e):
            run(e, "pool")


def build(S, L, dbg=False):
    _kw = dict(kind="ExternalOutput") if dbg else {}
    NB = S // T
    NT = S // 128
    nc = bass.Bass("TRN2", target_bir_lowering=False)
    xT = nc.dram_tensor("xT", [128, 8, S], F32, kind="ExternalInput").ap()
    wpk = nc.dram_tensor("wpk", [L, 128, WTOT], F32, kind="ExternalInput").ap()
    par_d = nc.dram_tensor("par", [128, NPL * L + 8], F32, kind="ExternalInput").ap()
    rope_d = nc.dram_tensor("rope", [128, 2, S], F32, kind="ExternalInput").ap()
    cst_d = nc.dram_tensor("cst", [128, 128], F32, kind="ExternalInput").ap()
    out_d = nc.dram_tensor("out", [128, 8, S], F32, kind="ExternalOutput").ap()
    xs = nc.dram_tensor("xs", [128, 8, S], F32, **_kw).ap()
    wbf = nc.dram_tensor("wbf", [L, 128, WTOT], BF16, **_kw).ap()
    ypre_s = nc.dram_tensor("ypre_s", [128, 2, S], BF16, **_kw).ap()
    hl_s = nc.dram_tensor("hl_s", [128, 2, S], BF16, **_kw).ap()
    QT_s = nc.dram_tensor("QT_s", [NH, 96, S], BF16, **_kw).ap()
    KT_s = nc.dram_tensor("KT_s", [NH, 64, S], BF16, **_kw).ap()
    kr_s = nc.dram_tensor("kr_s", [32, S], BF16, **_kw).ap()
    Vc_s = nc.dram_tensor("Vc_s", [128, NT, 4, 192], BF16, **_kw).ap()
    oT_s = nc.dram_tensor("oT_s", [4, 128, S], BF16, **_kw).ap()
    h2_s = nc.dram_tensor("h2_s", [128, 8, S], BF16, **_kw).ap()
    h1_s = nc.dram_tensor("h1_s", [128, 8, S], BF16, **_kw).ap()

    sc = Sched(nc)
    op = sc.op
    top = ExitStack()
    with top:
        _cnt = [0]

        def sbt(es, name, shape, dt):
            _cnt[0] += 1
            return es.enter_context(nc.sbuf_tensor(f"sb_{name}_{_cnt[0]}", shape, dt))

        ps_t = [top.enter_context(nc.psum_tensor(f"ps{i}", [128, T], F32)) for i in range(8)]
        psrr = [0]

        def ps():
            i = psrr[0]
            psrr[0] = (i + 1) % 8
            return ps_t[i], ("ps", i)

        NPAR = NPL * L + 8
        par = sbt(top, "par", [128, NPAR], F32)
        dpar = sbt(top, "dpar", [128, 32 * L], F32)
        cst = sbt(top, "cstf", [128, 128], F32)
        tri = sbt(top, "tri", [128, 128], BF16)
        ones = sbt(top, "ones", [128, 128], BF16)
        cb = sbt(top, "cb", [128, 4], F32)
        etmp = sbt(top, "etmp", [128, 2], F32)

        op("sp", lambda e: e.dma_start(out=par[:, :], in_=par_d[:, :]), [], ["par"], dma=True)
        op("sp", lambda e: e.dma_start(out=cst[:, :], in_=cst_d[:, :]), [], ["cst"], dma=True)
        op("dve", lambda e: e.tensor_copy(out=tri[:, :], in_=cst[:, :]), ["cst"], ["tri"])
        op("pool", lambda e: e.memset(ones[:, :], 1.0), [], ["ones"])
        op("pool", lambda e: e.memset(cb[:, 0:1], EPS), [], ["cb"])
        op("pool", lambda e: e.memset(cb[:, 1:2], 1.0), [], ["cb"])
        op("pool", lambda e: e.memset(cb[:, 2:3], -0.5), [], ["cb"])
        op("pool", lambda e: e.memset(cb[:, 3:4], 0.5), [], ["cb"])
        for l in range(L):
            b = l * NPL
            db = l * 32
            op("dve", lambda e, b=b, db=db: e.tensor_scalar(out=dpar[:, db:db + 4], in0=par[:, b + 25:b + 29], scalar1=0.5,
                                                           scalar2=None, op0=ALU.mult), ["par"], ["dpar"])
            op("dve", lambda e, b=b, db=db: e.tensor_scalar(out=dpar[:, db + 6:db + 30], in0=par[:, b + 31:b + 55], scalar1=0.5,
                                                           scalar2=None, op0=ALU.mult), ["par"], ["dpar"])
            op("act", lambda e, b=b: e.activation(out=etmp[:, :], in_=par[:, b + 29:b + 31], func=AF.Exp, scale=-1.0),
               ["par"], ["etmp"])
            op("act", lambda e: e.activation(out=etmp[:, :], in_=etmp[:, :], func=AF.Ln, bias=cb[:, 1:2], scale=1.0),
               ["etmp", "cb"], ["etmp"])
            op("dve", lambda e, db=db: e.tensor_scalar(out=dpar[:, db + 4:db + 6], in0=etmp[:, :], scalar1=-4.0,
                                                      scalar2=None, op0=ALU.mult), ["etmp"], ["dpar"])

        PSEGS = ["wina", "wq", "wk", "wv", "wbd", "wing", "womla", "wpp", "wlp", "wout", "w1a", "w2a", "w1b", "w2b"]

        def prepass_items(itemsl, after=()):
            for (l_, sname) in itemsl:
                o0, wdt = SEG[sname]
                op("pool", lambda e: e.dma_start(out=wbf[l_, :, o0:o0 + wdt], in_=wpk[l_, :, o0:o0 + wdt]),
                   list(after), [("wbf", l_, sname)], dma=True, pre=True)

        prepass_items([(0, n_) for n_ in PSEGS[0:5]])

        def load_w(tile3, l, sname, k, nch=1, only=None, kname=None):
            o0, wdt = SEG[sname]
            n = wdt // k
            cw = n // nch
            src = wbf[l, :, o0:o0 + wdt].rearrange("p (k n) -> p k n", k=k)
            for i in range(nch):
                if only is not None and i not in only:
                    continue
                op("sp", lambda e: e.dma_start(out=tile3[:, :, i * cw:(i + 1) * cw], in_=src[:, :, i * cw:(i + 1) * cw]),
                   [("wbf", l, sname)], [(kname or sname, i)], dma=True)

        def wkc(sname, c0, c1, cw):
            return [(sname, i) for i in range(c0 // cw, (c1 - 1) // cw + 1)]

        _dn = [0]

        def dump(name, tl, shape, dt, key):
            if not dbg:
                return
            _dn[0] += 1
            dd = nc.dram_tensor(f"dbg_{name}_{_dn[0]}", shape, dt, kind="ExternalOutput").ap()
            if len(shape) == 2:
                op("sp", lambda e: e.dma_start(out=dd[:, :], in_=tl[:, :]), [key], [], dma=True)
            else:
                op("sp", lambda e: e.dma_start(out=dd[:, :, :], in_=tl[:, :, :]), [key], [], dma=True)

        def rmsnorm(xb, xkey, nch, oi, gcol, sq, sqk, rs, rsk, rt, rtk, outt, outk, sq0=0):
            op("act", lambda e: e.activation(out=sq[:, sq0:sq0 + nch, :], in_=xb[:, 0:nch, :], func=AF.Square), [xkey], [sqk])
            pt, pk = ps()
            for c in range(nch):
                op("pe", lambda e, c=c: e.matmul(pt[:, :], lhsT=ones[:, :], rhs=sq[:, sq0 + c, :], start=(c == 0), stop=(c == nch - 1)),
                   [sqk, "ones"], [pk])
            op("act", lambda e: e.activation(out=rt[:, :], in_=pt[:, :], func=AF.Ln, bias=cb[:, 0:1], scale=oi), [pk, "cb"], [rtk])
            op("act", lambda e: e.activation(out=rs[:, :], in_=rt[:, :], func=AF.Exp, scale=-0.5), [rtk], [rsk])
            for c in range(nch):
                op("dve", lambda e, c=c: e.scalar_tensor_tensor(out=outt[:, c, :], in0=xb[:, c, :], scalar=par[:, gcol + c:gcol + c + 1],
                                                               in1=rs[:, :], op0=ALU.mult, op1=ALU.mult),
                   [xkey, rsk, "par"], [SK(outk, c)])

        def xsrc(l):
            return xT if l == 0 else xs

        for l in range(L):
            pb = l * NPL
            db = l * 32
            sc.barrier()
            with ExitStack() as es:
                wina = sbt(es, "wina", [128, 8, 1216], BF16)
                wq = sbt(es, "wq", [128, 3, 1024], BF16)
                wk = sbt(es, "wk", [128, 2, 512], BF16)
                wv = sbt(es, "wv", [128, 2, 512], BF16)
                wbd = sbt(es, "wbd", [128, 6, 128], BF16)
                xb = [sbt(es, f"xb{i}", [128, 8, T], F32) for i in range(2)]
                sq = sbt(es, "sq", [128, 8, T], BF16)
                hh_ = [sbt(es, f"h{i}", [128, 8, T], BF16) for i in range(2)]
                rsA = sbt(es, "rsA", [128, T], F32)
                rtA = sbt(es, "rtA", [128, T], F32)
                rsC = sbt(es, "rsC", [128, T], F32)
                rtC = sbt(es, "rtC", [128, T], F32)
                upool = sbt(es, "upool", [128, 2, 16 + T], F32)
                ulru = sbt(es, "ulru", [128, 2, 4 + T], F32)
                qlat = sbt(es, "qlat", [128, 3, T], F32)
                ckv = sbt(es, "ckv", [128, 2, T], F32)
                qn = sbt(es, "qn", [128, 3, T], BF16)
                ckvn = sbt(es, "ckvn", [128, 2, T], BF16)
                tab = [sbt(es, f"tab{i}", [128, 2, T], F32) for i in range(2)]
                t1 = sbt(es, "t1", [128, T], F32)
                t2 = sbt(es, "t2", [128, T], F32)
                kro = [sbt(es, f"kro{i}", [32, T], BF16) for i in range(2)]
                Kst = sbt(es, "Kst", [128, 4, T], BF16)
                Qn = sbt(es, "Qn", [128, 4, T], BF16)
                Qr = sbt(es, "Qr", [128, 2, T], BF16)
                Vst = sbt(es, "Vst", [128, 4, 4, 192], BF16)
                pa = sbt(es, "pa", [128, 2, 16 + T], F32)
                pbuf = sbt(es, "pbuf", [128, 2, 16 + T], F32)
                mixed = sbt(es, "mixed", [128, 2, T], BF16)
                ypre = sbt(es, "ypre", [128, 2, T], BF16)
                uc = sbt(es, "uc", [128, 2, T], F32)
                ucb = sbt(es, "ucb", [128, 2, T], BF16)
                tha = sbt(es, "tha", [128, 2, T], F32)
                thi = sbt(es, "thi", [128, 2, T], F32)
                bb = sbt(es, "bb", [128, 2, T], F32)
                hlf = sbt(es, "hlf", [128, 2, T], F32)
                hlb = sbt(es, "hlb", [128, 2, T], BF16)
                state = sbt(es, "state", [128, 2], F32)
                invw = sbt(es, "invw", [128, 2], F32)
                invc = sbt(es, "invc", [128, 2, 16], F32)

                op("sp", lambda e: e.dma_start(out=xb[0][:, :, :], in_=xsrc(l)[:, :, 0:T]), [("xs", 0)] if l > 0 else [], [("xb", 0)], dma=True)
                load_w(wina, l, "wina", 8, nch=4)
                op("sp", lambda e: e.dma_start(out=tab[0][:, :, :], in_=rope_d[:, :, 0:T]), [], [("tab", 0)], dma=True)
                load_w(wq, l, "wq", 3)
                load_w(wk, l, "wk", 2)
                load_w(wv, l, "wv", 2)
                load_w(wbd, l, "wbd", 6)
                op("pool", lambda e: e.memset(upool[:, :, 0:16], 0.0), [], ["upool"])
                op("pool", lambda e: e.memset(ulru[:, :, 0:4], 0.0), [], ["ulru"])
                op("pool", lambda e: e.memset(state[:, :], 0.0), [], ["state"])
                op("pool", lambda e: e.memset(Vst[:, :, :, 64:128], 1.0), [], ["Vst"])
                wins = [2, 4, 8, 16]
                for g in range(4):
                    p0, cc = (g % 2) * 64, g // 2
                    op("pool", lambda e: e.memset(invw[p0:p0 + 64, cc:cc + 1], 1.0 / wins[g]), [], ["invw"])
                    op("pool", lambda e: e.memset(invc[p0:p0 + 64, cc, :], 1.0 / wins[g]), [], ["invc"])
                    for t in range(wins[g] - 1):
                        op("pool", lambda e: e.memset(invc[p0:p0 + 64, cc, t:t + 1], 1.0 / (t + 1)), [], ["invc"])

                def load_x1(j):
                    src = xsrc(l)
                    op("sp", lambda e: e.dma_start(out=xb[j % 2][:, :, :], in_=src[:, :, j * T:(j + 1) * T]),
                       [("xs", j)] if l > 0 else [], [("xb", j % 2)], dma=True)

                def load_tab(j):
                    op("sp", lambda e: e.dma_start(out=tab[j % 2][:, :, :], in_=rope_d[:, :, j * T:(j + 1) * T]),
                       [], [("tab", j % 2)], dma=True)

                def stA(j):
                    i = j % 2
                    rmsnorm(xb[i], ("xb", i), 8, 1.0 / 1024, pb + 0, sq, "sq", rsA, "rsA", rtA, "rtA", hh_[i], ("h", i))
                    op("sp", lambda e: e.dma_start(out=h1_s[:, :, j * T:(j + 1) * T], in_=hh_[i][:, :, :]), [("h", i)], [("h1s", j)], dma=True)

                def stB(j):
                    i = j % 2
                    h = hh_[i]; hk = ("h", i)
                    TB = tab[i]; tk = ("tab", i)

                    def inproj(m0, mw):
                        pt, pk = ps()
                        for kc in range(8):
                            op("pe", lambda e: e.matmul(pt[0:mw, :], lhsT=wina[:, kc, m0:m0 + mw], rhs=h[:, kc, :],
                                                       start=(kc == 0), stop=(kc == 7)), [hk] + wkc("wina", m0, m0 + mw, 304), [pk])
                        return pt, pk

                    for c in range(3):
                        pt, pk = inproj(c * 128, 128)
                        op("act", lambda e: e.activation(out=qlat[:, c, :], in_=pt[:, :], func=AF.Copy), [pk], [SK("qlat", c)])
                    for c in range(2):
                        pt, pk = inproj(384 + c * 128, 128)
                        op("act", lambda e: e.activation(out=ckv[:, c, :], in_=pt[:, :], func=AF.Copy), [pk], [SK("ckv", c)])
                    pt, pk = inproj(640, 64)
                    op("dve", lambda e: e.tensor_tensor(out=t1[0:32, :], in0=pt[32:64, :], in1=TB[0:32, 1, :], op=ALU.mult), [pk, tk], ["t1"])
                    op("dve", lambda e: e.tensor_tensor(out=t2[0:32, :], in0=pt[0:32, :], in1=TB[0:32, 0, :], op=ALU.mult), [pk, tk], ["t2"])
                    op("dve", lambda e: e.tensor_tensor(out=kro[i][:, :], in0=t1[0:32, :], in1=t2[0:32, :], op=ALU.add), ["t1", "t2"], [("kro", i)])
                    op("sp", lambda e: e.dma_start(out=kr_s[:, j * T:(j + 1) * T], in_=kro[i][:, :]), [("kro", i)], [("krs", j)], dma=True)
                    for c in range(2):
                        pt, pk = inproj(704 + c * 128, 128)
                        op("act", lambda e: e.activation(out=upool[:, c, 16:16 + T], in_=pt[:, :], func=AF.Copy), [pk], [SK("upool", c)])
                    for c in range(2):
                        pt, pk = inproj(960 + c * 128, 128)
                        op("act", lambda e: e.activation(out=ulru[:, c, 4:4 + T], in_=pt[:, :], func=AF.Copy), [pk], [SK("ulru", c)])

                def stC(j):
                    rmsnorm(qlat, "qlat", 3, 1.0 / 384, pb + 10, sq, "sq", rsC, "rsC", rtC, "rtC", qn, "qn", sq0=0)
                    rmsnorm(ckv, "ckv", 2, 1.0 / 256, pb + 13, sq, "sq", rsC, "rsC", rtC, "rtC", ckvn, "ckvn", sq0=3)

                def stD(j):
                    i = j % 2
                    TB = tab[i]; tk = ("tab", i)
                    for p in range(4):
                        pt, pk = ps()
                        for kc in range(2):
                            op("pe", lambda e: e.matmul(pt[:, :], lhsT=wk[:, kc, p * 128:(p + 1) * 128], rhs=ckvn[:, kc, :],
                                                       start=(kc == 0), stop=(kc == 1)), ["ckvn", ("wk", 0)], [pk])
                        op("act", lambda e: e.activation(out=Kst[:, p, :], in_=pt[:, :], func=AF.Copy), [pk], [SK("Kst", p)])
                    ktv = KT_s.rearrange("(q e) p t -> e p q t", e=2)
                    for ev in range(2):
                        op("sp", lambda e: e.dma_start(out=ktv[ev, :, :, j * T:(j + 1) * T], in_=Kst[64 * ev:64 * ev + 64, :, :]),
                           ["Kst"], [("KT", j, ev)], dma=True)
                    for tt in range(4):
                        pt, pk = ps()
                        for kc in range(2):
                            op("pe", lambda e: e.matmul(pt[:, :], lhsT=ckvn[:, kc, tt * 128:(tt + 1) * 128], rhs=wv[:, kc, :],
                                                       start=(kc == 0), stop=(kc == 1)), ["ckvn", ("wv", 0)], [pk])
                        pv = pt[:, :].rearrange("p (q e c) -> p q e c", q=4, e=2)
                        op("act", lambda e: e.activation(out=Vst[:, tt, :, 0:64], in_=pv[:, :, 0, :], func=AF.Copy), [pk], [SK("Vst", (tt, 0))])
                        op("dve", lambda e: e.tensor_copy(out=Vst[:, tt, :, 128:192], in_=pv[:, :, 1, :]), [pk], [SK("Vst", (tt, 1))])
                    op("sp", lambda e: e.dma_start(out=Vc_s[:, 4 * j:4 * j + 4, :, :], in_=Vst[:, :, :, :]), ["Vst"], [("Vc", j)], dma=True)
                    def qmm(c0):
                        pt, pk = ps()
                        for kc in range(3):
                            op("pe", lambda e: e.matmul(pt[:, :], lhsT=wq[:, kc, c0:c0 + 128], rhs=qn[:, kc, :],
                                                       start=(kc == 0), stop=(kc == 2)), ["qn", ("wq", 0)], [pk])
                        return pt, pk
                    for g in range(2):
                        ptr, pkr = qmm(512 + g * 128)
                        ptp, pkp = qmm(768 + g * 128)
                        op("dve", lambda e: e.tensor_tensor(out=t1[:, :], in0=ptp[:, :], in1=TB[:, 1, :], op=ALU.mult), [pkp, tk], ["t1"])
                        op("dve", lambda e: e.tensor_tensor(out=t2[:, :], in0=ptr[:, :], in1=TB[:, 0, :], op=ALU.mult), [pkr, tk], ["t2"])
                        op("dve", lambda e: e.tensor_tensor(out=Qr[:, g, :], in0=t1[:, :], in1=t2[:, :], op=ALU.add), ["t1", "t2"], [SK("Qr", g)])
                    for p in range(4):
                        pt, pk = qmm(p * 128)
                        op("act", lambda e: e.activation(out=Qn[:, p, :], in_=pt[:, :], func=AF.Copy), [pk], [SK("Qn", p)])
                    qtv = QT_s.rearrange("(q e) p t -> e p q t", e=2)
                    for ev in range(2):
                        op("sp", lambda e: e.dma_start(out=qtv[ev, 0:64, :, j * T:(j + 1) * T], in_=Qn[64 * ev:64 * ev + 64, :, :]),
                           ["Qn"], [("QT", j, ev)], dma=True)
                    qrv = QT_s.rearrange("(g hq) p t -> hq p g t", hq=4)
                    for hq in range(4):
                        op("sp", lambda e: e.dma_start(out=qrv[hq, 64:96, :, j * T:(j + 1) * T], in_=Qr[32 * hq:32 * hq + 32, :, :]),
                           ["Qr"], [("QT", j, 2 + hq)], dma=True)

                def stE1(j):
                    W = 16 + T
                    op("pool", lambda e: e.tensor_tensor(out=pa[:, :, 1:W], in0=upool[:, :, 1:W], in1=upool[:, :, 0:W - 1], op=ALU.add), ["upool"], ["pa"])
                    op("pool", lambda e: e.tensor_tensor(out=pbuf[:, :, 3:W], in0=pa[:, :, 3:W], in1=pa[:, :, 1:W - 2], op=ALU.add), ["pa"], ["pbuf"])
                    op("pool", lambda e: e.tensor_tensor(out=pa[:, 1, 7:W], in0=pbuf[:, 1, 7:W], in1=pbuf[:, 1, 3:W - 4], op=ALU.add), ["pbuf"], ["pa"])
                    op("pool", lambda e: e.tensor_tensor(out=pbuf[64:128, 1, 16:W], in0=pa[64:128, 1, 16:W], in1=pa[64:128, 1, 8:W - 8], op=ALU.add), ["pa", "pbuf"], ["pbuf"])
                    for g in range(4):
                        p0, c = (g % 2) * 64, g // 2
                        srcb = pa if g % 2 == 0 else pbuf
                        op("dve", lambda e: e.scalar_tensor_tensor(out=mixed[p0:p0 + 64, c, :], in0=srcb[p0:p0 + 64, c, 16:W], scalar=invw[p0:p0 + 64, c:c + 1],
                                                                  in1=upool[p0:p0 + 64, c, 16:W], op0=ALU.mult, op1=ALU.subtract),
                           ["pa", "pbuf", "upool", "invw"], [SK("mixed", g)])
                        if j == 0:
                            op("dve", lambda e: e.tensor_tensor(out=t1[p0:p0 + 64, 0:16], in0=srcb[p0:p0 + 64, c, 16:32], in1=invc[p0:p0 + 64, c, :], op=ALU.mult),
                               ["pa", "pbuf", "invc"], ["t1"])
                            op("dve", lambda e: e.tensor_tensor(out=mixed[p0:p0 + 64, c, 0:16], in0=t1[p0:p0 + 64, 0:16], in1=upool[p0:p0 + 64, c, 16:32], op=ALU.subtract),
                               ["t1", "upool"], [SK("mixed", g)])
                    op("pool", lambda e: e.tensor_copy(out=upool[:, :, 0:16], in_=upool[:, :, T:T + 16]), ["upool", "mixed"], ["upool"])
                    cw = pb + 15
                    for c in range(2):
                        op("dve", lambda e: e.tensor_scalar(out=uc[:, c, :], in0=ulru[:, c, 1:1 + T], scalar1=par[:, cw + c:cw + c + 1],
                                                           scalar2=par[:, pb + 23 + c:pb + 24 + c], op0=ALU.mult, op1=ALU.add),
                           ["ulru", "par"], [SK("uc", c)])
                        for k in range(1, 4):
                            op("dve", lambda e: e.scalar_tensor_tensor(out=uc[:, c, :], in0=ulru[:, c, 1 + k:1 + k + T],
                                                                      scalar=par[:, cw + 2 * k + c:cw + 2 * k + c + 1],
                                                                      in1=uc[:, c, :], op0=ALU.mult, op1=ALU.add),
                               ["ulru", "par", SK("uc", c)], [SK("uc", c)])
                    op("act", lambda e: e.activation(out=ucb[:, :, :], in_=uc[:, :, :], func=AF.Copy), ["uc"], ["ucb"])
                    op("pool", lambda e: e.tensor_copy(out=ulru[:, :, 0:4], in_=ulru[:, :, T:T + 4]), ["ulru", "uc"], ["ulru"])

                def stE2(j):
                    for c in range(2):
                        pt, pk = ps()
                        op("pe", lambda e: e.matmul(pt[:, :], lhsT=wbd[:, c, :], rhs=mixed[:, c, :], start=True, stop=True), ["mixed", ("wbd", 0)], [pk])
                        op("dve", lambda e: e.tensor_scalar(out=ypre[:, c, :], in0=pt[:, :], scalar1=par[:, pb + 8 + c:pb + 9 + c], scalar2=None,
                                                           op0=ALU.mult), [pk, "par"], [SK("ypre", c)])
                    op("sp", lambda e: e.dma_start(out=ypre_s[:, :, j * T:(j + 1) * T], in_=ypre[:, :, :]), ["ypre"], [("ypre", j)], dma=True)
                    for c in range(2):
                        pt, pk = ps()
                        op("pe", lambda e: e.matmul(pt[:, :], lhsT=wbd[:, 2 + c, :], rhs=ucb[:, c, :], start=True, stop=True), ["ucb", ("wbd", 0)], [pk])
                        op("act", lambda e: e.activation(out=tha[:, c, :], in_=pt[:, :], func=AF.Tanh, bias=dpar[:, db + c:db + c + 1], scale=0.5),
                           [pk, "dpar"], [SK("tha", c)])
                        pt2, pk2 = ps()
                        op("pe", lambda e: e.matmul(pt2[:, :], lhsT=wbd[:, 4 + c, :], rhs=ucb[:, c, :], start=True, stop=True), ["ucb", ("wbd", 0)], [pk2])
                        op("act", lambda e: e.activation(out=thi[:, c, :], in_=pt2[:, :], func=AF.Tanh, bias=dpar[:, db + 2 + c:db + 3 + c], scale=0.5),
                           [pk2, "dpar"], [SK("thi", c)])
                        op("act", lambda e: e.activation(out=tha[:, c, :], in_=tha[:, c, :], func=AF.Exp, bias=dpar[:, db + 4 + c:db + 5 + c],
                                                        scale=dpar[:, db + 4 + c:db + 5 + c]), [SK("tha", c), "dpar"], [SK("tha", c)])
                    op("dve", lambda e: e.scalar_tensor_tensor(out=bb[:, :, :], in0=tha[:, :, :], scalar=0.99999997, in1=tha[:, :, :],
                                                              op0=ALU.min, op1=ALU.mult), ["tha"], ["bb"])
                    op("act", lambda e: e.activation(out=bb[:, :, :], in_=bb[:, :, :], func=AF.Ln, bias=cb[:, 1:2], scale=-1.0), ["bb", "cb"], ["bb"])
                    op("act", lambda e: e.activation(out=bb[:, :, :], in_=bb[:, :, :], func=AF.Exp, scale=0.5), ["bb"], ["bb"])
                    op("act", lambda e: e.activation(out=thi[:, :, :], in_=thi[:, :, :], func=AF.Identity, bias=cb[:, 3:4], scale=0.5), ["thi", "cb"], ["thi"])

                def stE2b(j):
                    op("dve", lambda e: e.tensor_tensor(out=thi[:, :, :], in0=thi[:, :, :], in1=uc[:, :, :], op=ALU.mult), ["thi", "uc"], ["thi"])
                    op("dve", lambda e: e.tensor_tensor(out=bb[:, :, :], in0=bb[:, :, :], in1=thi[:, :, :], op=ALU.mult), ["bb", "thi"], ["bb"])
                    for c in range(2):
                        op("dve", lambda e: e.tensor_tensor_scan(out=hlf[:, c, :], data0=tha[:, c, :], data1=bb[:, c, :], initial=state[:, c:c + 1],
                                                                op0=ALU.mult, op1=ALU.add), ["tha", "bb", "state"], [SK("hlf", c)])
                    op("dve", lambda e: e.tensor_copy(out=state[:, :], in_=hlf[:, :, T - 1]), ["hlf"], ["state"])
                    op("act", lambda e: e.activation(out=hlb[:, :, :], in_=hlf[:, :, :], func=AF.Copy), ["hlf"], ["hlb"])
                    op("sp", lambda e: e.dma_start(out=hl_s[:, :, j * T:(j + 1) * T], in_=hlb[:, :, :]), ["hlb"], [("hl", j)], dma=True)

                if NB > 1:
                    load_x1(1)
                    load_tab(1)
                stA(0)
                stB(0)
                if NB > 1:
                    stA(1)
                for j in range(NB):
                    if j + 2 < NB:
                        load_x1(j + 2)
                    stC(j)
                    stE1(j)
                    if j + 1 < NB:
                        stB(j + 1)
                    if j + 2 < NB:
                        stA(j + 2)
                    stE2(j)
                    stD(j)
                    if j + 2 < NB:
                        load_tab(j + 2)
                    stE2b(j)

            sc.barrier()
            with ExitStack() as es:
                Kh = [sbt(es, f"Kh{i}", [96, S], BF16) for i in range(2)]
                Qh = [sbt(es, f"Qh{i}", [96, S], BF16) for i in range(2)]
                Vp = [sbt(es, f"Vp{i}", [128, NT, 192], BF16) for i in range(2)]
                oT = [sbt(es, f"oT{i}", [128, S], BF16) for i in range(2)]
                NP_ = 6
                Pt = [sbt(es, f"P{i}", [128, T], BF16) for i in range(NP_)]
                rec = [sbt(es, f"rec{i}", [128, T], F32) for i in range(2)]
                allj = list(range(NB))
                scale = 96.0 ** -0.5
                LA = 3
                pend = ([(0, n_) for n_ in PSEGS[5:]] if l == 0 else []) + ([(l + 1, n_) for n_ in PSEGS] if l + 1 < L else [])
                pchunks = [pend[i_:i_ + 7] for i_ in range(0, len(pend), 7)]
                items = [(hh, j, kt) for hh in range(NH) for j in range(NB) for kt in range(4 * j + 4)]
                NI = len(items)

                def bufs(hh):
                    p = hh // 2
                    return (Kh[hh % 2], Qh[hh % 2], Vp[p % 2], oT[p % 2], ("Kh", hh % 2), ("Qh", hh % 2), ("Vp", p % 2), ("oT", p % 2))

                def load_head(h2):
                    K2, Q2, V2, O2, kk2, qk2, vk2, ok2 = bufs(h2)
                    op("sp", lambda e: e.dma_start(out=K2[0:64, :], in_=KT_s[h2, :, :]), [("KT", jj, ev_) for jj in allj for ev_ in range(2)], [SK(kk2, 0)], dma=True)
                    op("sp", lambda e: e.dma_start(out=K2[64:96, :], in_=kr_s[:, :]), [("krs", jj) for jj in allj], [SK(kk2, 1)], dma=True)
                    op("sp", lambda e: e.dma_start(out=Q2[:, :], in_=QT_s[h2, :, :]), [("QT", jj, x_) for jj in allj for x_ in range(6)], [qk2], dma=True)

                def load_v(p2):
                    V2 = Vp[p2 % 2]
                    op("sp", lambda e: e.dma_start(out=V2[:, :, :], in_=Vc_s[:, :, p2, :]), [("Vc", jj) for jj in allj], [("Vp", p2 % 2)], dma=True)

                def qk(ii):
                    hh, j, kt = items[ii]
                    p, ev = hh // 2, hh % 2
                    K_, Q_, V_, O_, kk, qk_, vk, ok = bufs(hh)
                    if j == 0 and kt == 0:
                        if hh == 0:
                            load_head(0)
                            load_v(0)
                        if hh + 1 < NH:
                            load_head(hh + 1)
                        if hh % 2 == 0 and (hh // 2) < len(pchunks):
                            prepass_items(pchunks[hh // 2], after=[("Kh", 0), ("Qh", 0), ("Vp", p % 2), ("Kh", 1), ("Qh", 1)])
                        if ev == 1 and p + 1 < 4:
                            load_v(p + 1)
                    dd = kt - 4 * j
                    c0 = 128 * dd if dd >= 0 else 0
                    st = ps_t[ii % 4]; sk = ("ps", ii % 4)
                    Pm = Pt[ii % NP_]; pk_ = ("P", ii % NP_)
                    op("pe", lambda e: e.matmul(st[:, c0:T], lhsT=K_[:, kt * 128:(kt + 1) * 128], rhs=Q_[:, j * T + c0:(j + 1) * T], start=True, stop=True),
                       [kk, qk_], [sk])
                    op("act", lambda e: e.activation(out=Pm[:, c0:T], in_=st[:, c0:T], func=AF.Exp, scale=scale), [sk], [pk_])
                    if dd >= 0:
                        op("dve", lambda e: e.tensor_tensor(out=Pm[:, c0:c0 + 128], in0=Pm[:, c0:c0 + 128], in1=tri[:, :], op=ALU.mult),
                           [pk_, "tri"], [pk_])

                def pv(ii):
                    hh, j, kt = items[ii]
                    p, ev = hh // 2, hh % 2
                    K_, Q_, V_, O_, kk, qk_, vk, ok = bufs(hh)
                    dd = kt - 4 * j
                    c0 = 128 * dd if dd >= 0 else 0
                    nk = 4 * j + 4
                    Pm = Pt[ii % NP_]; pk_ = ("P", ii % NP_)
                    acc = ps_t[4 + (j % 4)]; ak = ("ps", 4 + (j % 4))
                    op("pe", lambda e: e.matmul(acc[:, c0:T], lhsT=V_[:, kt, ev * 64:ev * 64 + 128], rhs=Pm[:, c0:T], start=(kt == 0), stop=(kt == nk - 1)),
                       [vk, pk_], [ak])
                    if kt == nk - 1:
                        vb, dbs = (0, 64) if ev == 0 else (64, 0)
                        R = rec[j % 2]; rk = ("rec", j % 2)
                        op("dve", lambda e: e.reciprocal(out=R[dbs:dbs + 64, :], in_=acc[dbs:dbs + 64, :]), [ak], [rk])
                        op("dve", lambda e: e.tensor_tensor(out=O_[vb:vb + 64, j * T:(j + 1) * T], in0=acc[vb:vb + 64, :], in1=R[dbs:dbs + 64, :], op=ALU.mult),
                           [ak, rk], [SK(ok, (ev, j))])
                        if j == NB - 1 and ev == 1:
                            op("sp", lambda e: e.dma_start(out=oT_s[p, :, :], in_=O_[:, :]), [ok], [("oTs", p)], dma=True)

                for ii in range(NI + LA):
                    if ii < NI:
                        qk(ii)
                    if ii - LA >= 0:
                        pv(ii - LA)

            sc.barrier()
            with ExitStack() as es:
                wing = sbt(es, "wing", [128, 8, 3072], BF16)
                womla = sbt(es, "womla", [128, 4, 1024], BF16)
                wpp = sbt(es, "wpp", [128, 2, 1024], BF16)
                wlp = sbt(es, "wlp", [128, 2, 1024], BF16)
                wout = sbt(es, "wout", [128, 8, 1024], BF16)
                xb = [sbt(es, f"xb3_{i}", [128, 8, T], F32) for i in range(2)]
                ypb = [sbt(es, f"ypb{i}", [128, 2, T], BF16) for i in range(2)]
                hlb3 = [sbt(es, f"hlb3_{i}", [128, 2, T], BF16) for i in range(2)]
                otb = [sbt(es, f"otb{i}", [128, 4, T], BF16) for i in range(2)]
                hb = [sbt(es, f"h3_{i}", [128, 8, T], BF16) for i in range(2)]
                th = [sbt(es, f"th{i}", [128, T], BF16) for i in range(6)]
                mt = [sbt(es, f"mt{i}", [128, T], F32) for i in range(6)]
                merged = sbt(es, "merged", [128, 8, T], BF16)

                def load_x3(j, first=False):
                    src = xsrc(l)
                    i = j % 2
                    op("sp", lambda e: e.dma_start(out=hb[i][:, :, :], in_=h1_s[:, :, j * T:(j + 1) * T]), [("h1s", j)], [("hb", i)], dma=True)
                    if first:
                        load_w(wing, l, "wing", 8, nch=4, only=[0])
                        load_w(wpp, l, "wpp", 2)
                    op("sp", lambda e: e.dma_start(out=ypb[i][:, :, :], in_=ypre_s[:, :, j * T:(j + 1) * T]), [("ypre", j)], [("ypb", i)], dma=True)
                    if first:
                        load_w(womla, l, "womla", 4)
                    op("sp", lambda e: e.dma_start(out=otb[i][:, :, :], in_=oT_s[:, :, j * T:(j + 1) * T].rearrange("q p t -> p q t")),
                       [("oTs", p) for p in range(4)], [("otb", i)], dma=True)
                    if first:
                        load_w(wlp, l, "wlp", 2)
                    op("sp", lambda e: e.dma_start(out=hlb3[i][:, :, :], in_=hl_s[:, :, j * T:(j + 1) * T]), [("hl", j)], [("hlb3", i)], dma=True)
                    op("sp", lambda e: e.dma_start(out=xb[i][:, :, :], in_=src[:, :, j * T:(j + 1) * T]),
                       [("xs", j)] if l > 0 else [], [("xb", i)], dma=True)

                load_x3(0, first=True)
                load_w(wing, l, "wing", 8, nch=4, only=[1, 2, 3])
                load_w(wout, l, "wout", 8)
                trr = 0
                for j in range(NB):
                    if j + 1 < NB:
                        load_x3(j + 1)
                    i = j % 2
                    X = xb[i]; xk = ("xb", i)
                    h = hb[i]; hkey = ("hb", i)
                    for c in range(8):
                        srcs = [(wpp, ("wpp", 0), ypb[i], ("ypb", i), 2), (womla, ("womla", 0), otb[i], ("otb", i), 4), (wlp, ("wlp", 0), hlb3[i], ("hlb3", i), 2)]
                        mts = []
                        for b in range(3):
                            TH = th[trr % 6]; thk = ("th", trr % 6)
                            MT = mt[trr % 6]; mtk = ("mt", trr % 6)
                            trr += 1
                            pt, pk = ps()
                            for kc in range(8):
                                op("pe", lambda e, kc=kc, b=b, c=c, pt=pt: e.matmul(pt[:, :], lhsT=wing[:, kc, (c * 3 + b) * 128:(c * 3 + b + 1) * 128], rhs=h[:, kc, :],
                                                                                 start=(kc == 0), stop=(kc == 7)), [hkey, ("wing", c // 2)], [pk])
                            gc = db + 6 + b * 8 + c
                            op("act", lambda e, pt=pt, TH=TH, gc=gc: e.activation(out=TH[:, :], in_=pt[:, :], func=AF.Tanh, bias=dpar[:, gc:gc + 1], scale=0.5),
                               [pk, "dpar"], [thk])
                            wt_, wkey, src, skey, nk = srcs[b]
                            pt2, pk2 = ps()
                            for kc in range(nk):
                                op("pe", lambda e, kc=kc, c=c, pt2=pt2, wt_=wt_, src=src, nk=nk: e.matmul(pt2[:, :], lhsT=wt_[:, kc, c * 128:(c + 1) * 128], rhs=src[:, kc, :],
                                                                                                     start=(kc == 0), stop=(kc == nk - 1)), [skey, wkey], [pk2])
                            op("dve", lambda e, TH=TH, MT=MT, pt2=pt2: e.scalar_tensor_tensor(out=MT[:, :], in0=TH[:, :], scalar=1.0, in1=pt2[:, :],
                                                                                           op0=ALU.add, op1=ALU.mult), [thk, pk2], [mtk])
                            mts.append((MT, mtk))
                        op("pool", lambda e, a=mts[0][0], b_=mts[1][0]: e.tensor_tensor(out=a[:, :], in0=a[:, :], in1=b_[:, :], op=ALU.add),
                           [mts[0][1], mts[1][1]], [mts[0][1]])
                        op("pool", lambda e, a=mts[0][0], b_=mts[2][0], c=c: e.tensor_tensor(out=merged[:, c, :], in0=a[:, :], in1=b_[:, :], op=ALU.add),
                           [mts[0][1], mts[2][1]], [("merged", c)])
                    for m in range(8):
                        pt, pk = ps()
                        for kc in range(8):
                            op("pe", lambda e, kc=kc, m=m, pt=pt: e.matmul(pt[:, :], lhsT=wout[:, kc, m * 128:(m + 1) * 128], rhs=merged[:, kc, :],
                                                                        start=(kc == 0), stop=(kc == 7)), [("merged", kc), ("wout", 0)], [pk])
                        op("dve", lambda e, m=m, pt=pt, X=X: e.scalar_tensor_tensor(out=X[:, m, :], in0=pt[:, :], scalar=0.5, in1=X[:, m, :],
                                                                                 op0=ALU.mult, op1=ALU.add), [pk, SK(xk, m)], [SK(xk, m)])
                    op("sp", lambda e, j=j, X=X: e.dma_start(out=xs[:, :, j * T:(j + 1) * T], in_=X[:, :, :]), [xk], [("xs", j)], dma=True)

            sc.barrier()
            with ExitStack() as es:
                w1 = sbt(es, "w1", [128, 8, 2048], BF16)
                w2 = sbt(es, "w2", [128, 16, 1024], BF16)
                xb = [sbt(es, f"xb4_{i}", [128, 8, T], F32) for i in range(2)]
                h2 = [sbt(es, f"h2_{i}", [128, 8, T], BF16) for i in range(2)]
                sq = sbt(es, "sq4", [128, 8, T], BF16)
                rs = sbt(es, "rs4", [128, T], F32)
                rt = sbt(es, "rt4", [128, T], F32)
                hid = sbt(es, "hid", [128, 16, T], BF16)
                rl = [sbt(es, f"rl{i}", [128, T], F32) for i in range(3)]
                ob = [sbt(es, f"ob{i}", [128, 8, T], F32) for i in range(2)] if l == L - 1 else None
                for half in range(2):
                    last = (l == L - 1 and half == 1)
                    w1n = "w1a" if half == 0 else "w1b"
                    w2n = "w2a" if half == 0 else "w2b"

                    def load_x4(j, first=False):
                        i = j % 2
                        if half == 1:
                            op("sp", lambda e: e.dma_start(out=h2[i][:, :, :], in_=h2_s[:, :, j * T:(j + 1) * T]), [("h2s", j)], [("h2", i)], dma=True)
                            if first:
                                load_w(w1, l, w1n, 8, nch=4, only=[0], kname="w1")
                        op("sp", lambda e: e.dma_start(out=xb[i][:, :, :], in_=xs[:, :, j * T:(j + 1) * T]), [("xs", j)], [("xb", i)], dma=True)
                        if half == 0 and first:
                            load_w(w1, l, w1n, 8, nch=4, only=[0], kname="w1")

                    load_x4(0, first=True)
                    load_w(w1, l, w1n, 8, nch=4, only=[1, 2, 3], kname="w1")
                    load_w(w2, l, w2n, 16, kname="w2")

                    def norm4(j):
                        i = j % 2
                        rmsnorm(xb[i], ("xb", i), 8, 1.0 / 1024, pb + 55, sq, "sq", rs, "rs", rt, "rt", h2[i], ("h2", i))
                        op("sp", lambda e: e.dma_start(out=h2_s[:, :, j * T:(j + 1) * T], in_=h2[i][:, :, :]), [("h2", i)], [("h2s", j)], dma=True)

                    if half == 0:
                        norm4(0)
                    rrr = 0
                    for j in range(NB):
                        if j + 1 < NB:
                            load_x4(j + 1)
                        i = j % 2
                        X = xb[i]; xk = ("xb", i)
                        H = h2[i]; hk = ("h2", i)
                        for f in range(16):
                            pt, pk = ps()
                            for kc in range(8):
                                op("pe", lambda e, kc=kc, f=f, pt=pt, H=H: e.matmul(pt[:, :], lhsT=w1[:, kc, f * 128:(f + 1) * 128], rhs=H[:, kc, :],
                                                                                 start=(kc == 0), stop=(kc == 7)), [hk, ("w1", f // 4)], [pk])
                            RL = rl[rrr % 3]; rlk = ("rl", rrr % 3); rrr += 1
                            op("act", lambda e, pt=pt, RL=RL: e.activation(out=RL[:, :], in_=pt[:, :], func=AF.Relu), [pk], [rlk])
                            op("dve" if f % 2 == 0 else "pool", lambda e, RL=RL, f=f: e.tensor_tensor(out=hid[:, f, :], in0=RL[:, :], in1=RL[:, :], op=ALU.mult),
                               [rlk], [("hid", f)])
                        if half == 0 and j + 1 < NB:
                            norm4(j + 1)
                        for m in range(8):
                            pt, pk = ps()
                            for kc in range(16):
                                op("pe", lambda e, kc=kc, m=m, pt=pt: e.matmul(pt[:, :], lhsT=w2[:, kc, m * 128:(m + 1) * 128], rhs=hid[:, kc, :],
                                                                            start=(kc == 0), stop=(kc == 15)), [("hid", kc), ("w2", 0)], [pk])
                            op("dve", lambda e, m=m, pt=pt, X=X: e.tensor_tensor(out=X[:, m, :], in0=pt[:, :], in1=X[:, m, :], op=ALU.add), [pk, SK(xk, m)], [SK(xk, m)])
                        if not last:
                            op("sp", lambda e, j=j, X=X: e.dma_start(out=xs[:, :, j * T:(j + 1) * T], in_=X[:, :, :]), [xk], [("xs", j)], dma=True)
                        else:
                            O = ob[i]; okk = ("ob", i)
                            rmsnorm(X, xk, 8, 1.0 / 1024, NPL * L, sq, "sq", rs, "rs", rt, "rt", O, okk)
                            op("sp", lambda e, j=j, O=O: e.dma_start(out=out_d[:, :, j * T:(j + 1) * T], in_=O[:, :, :]), [okk], [("out", j)], dma=True)

        sc.barrier()
        sc.emit(top)
    return nc


def _pk(Wm):
    K, N = Wm.shape
    kc = K // 128
    return np.ascontiguousarray(Wm.reshape(kc, 128, N).transpose(1, 0, 2).reshape(128, kc * N))


def _bd(w4):
    o = np.zeros((128, 2, 128), np.float32)
    for g in range(4):
        p0, c = (g % 2) * 64, g // 2
        o[p0:p0 + 64, c, p0:p0 + 64] = w4[g]
    return o.reshape(128, 256)


def pack_host(inp, S, L):
    f = lambda a: np.asarray(a, dtype=np.float32)
    wpk = np.zeros((L, 128, WTOT), np.float32)
    par = np.zeros((128, NPL * L + 8), np.float32)
    perm = np.concatenate([np.arange(16, 32), np.arange(0, 16)])
    for l in range(L):
        w_in = f(inp["w_in"][l])
        kr = w_in[:, 896:928]
        wina = np.concatenate([w_in[:, 256:640], w_in[:, 640:896], kr, kr[:, perm], w_in[:, 0:256], w_in[:, 928:1184]], axis=1)
        assert wina.shape[1] == 1216
        wq0 = f(inp["w_q_up"][l]).reshape(384, 8, 96)
        wq = np.concatenate([wq0[:, :, 0:64].reshape(384, 512), wq0[:, :, 64:96].reshape(384, 256),
                             wq0[:, :, 64 + perm].reshape(384, 256)], axis=1)
        wkv = f(inp["w_kv_up"][l]).reshape(256, 8, 128)
        wk = np.ascontiguousarray(wkv[:, :, 0:64]).reshape(256, 512)
        wv = np.ascontiguousarray(wkv[:, :, 64:128]).reshape(256, 512)
        wbd = np.concatenate([_bd(f(inp["w_pool_grp"][l])), _bd(f(inp["w_lru_a"][l])), _bd(f(inp["w_lru_x"][l]))], axis=1)
        w1 = f(inp["w_ff1"][l]); w2 = f(inp["w_ff2"][l])
        segs = {"wina": _pk(wina), "wq": _pk(wq), "wk": _pk(wk), "wv": _pk(wv), "wbd": wbd,
                "wing": _pk(np.ascontiguousarray(w_in[:, 1184:].reshape(1024, 3, 8, 128).transpose(0, 2, 1, 3)).reshape(1024, 3072)), "womla": _pk(f(inp["w_mla_o"][l])), "wpp": _pk(f(inp["w_pool_proj"][l])),
                "wlp": _pk(f(inp["w_lru_proj"][l])), "wout": _pk(f(inp["w_out"][l])),
                "w1a": _pk(w1[:, 0:2048]), "w2a": _pk(w2[0:2048, :]), "w1b": _pk(w1[:, 2048:]), "w2b": _pk(w2[2048:, :])}
        for n, a in segs.items():
            o0, wdt = SEG[n]
            assert a.shape == (128, wdt), (n, a.shape, wdt)
            wpk[l, :, o0:o0 + wdt] = a
        b = l * NPL
        col = lambda v: f(v).reshape(-1, 128).T
        par[:, b + 0:b + 8] = col(inp["g_mix"][l])
        par[:, b + 8:b + 10] = col(inp["pool_scale"][l])
        par[:, b + 10:b + 13] = col(inp["g_q"][l])
        par[:, b + 13:b + 15] = col(inp["g_kv"][l])
        par[:, b + 15:b + 23] = col(inp["conv_w"][l])
        par[:, b + 23:b + 25] = col(inp["conv_b"][l])
        par[:, b + 25:b + 27] = col(inp["b_lru_a"][l])
        par[:, b + 27:b + 29] = col(inp["b_lru_x"][l])
        par[:, b + 29:b + 31] = col(inp["lru_lambda"][l])
        par[:, b + 31:b + 55] = col(inp["b_gate"][l])
        par[:, b + 55:b + 63] = col(inp["g_ffn"][l])
    par[:, NPL * L:NPL * L + 8] = f(inp["g_final"]).reshape(-1, 128).T
    pos = np.arange(S, dtype=np.float32)
    inv = (np.float32(10000.0) ** (-np.arange(0, 32, 2, dtype=np.float32) / np.float32(32))).astype(np.float32)
    ang = (pos[:, None] * inv[None, :]).astype(np.float32)
    c = np.cos(ang).T.astype(np.float32); s = np.sin(ang).T.astype(np.float32)
    rope = np.zeros((128, 2, S), np.float32)
    for r_ in range(4):
        rope[32 * r_:32 * r_ + 16, 0] = c; rope[32 * r_ + 16:32 * r_ + 32, 0] = c
        rope[32 * r_:32 * r_ + 16, 1] = -s; rope[32 * r_ + 16:32 * r_ + 32, 1] = s
    tri = (np.arange(128)[None, :] >= np.arange(128)[:, None]).astype(np.float32)
    return wpk, par, rope, tri


_NC_CACHE = {}


def run(inp, S, L, B):
    wpk, par, rope, tri = pack_host(inp, S, L)
    x = np.asarray(inp["x"], dtype=np.float32)
    key = (S, L)
    if key not in _NC_CACHE:
        _NC_CACHE[key] = build(S, L)
    nc = _NC_CACHE[key]
    in_maps = []
    for b in range(B):
        xTb = np.ascontiguousarray(x[b].reshape(S, 8, 128).transpose(2, 1, 0))
        in_maps.append({"xT": xTb, "wpk": wpk, "par": par, "rope": rope, "cst": tri})
    res = run_bass_kernel_spmd(nc, in_maps, core_ids=list(range(B)))
    outs = []
    for b in range(B):
        o = np.asarray(res.results[b]["out"], dtype=np.float32)
        outs.append(o.transpose(2, 1, 0).reshape(S, D))
    return np.stack(outs, axis=0)


def kernel(**inputs):
    x = inputs["x"]
    B, S, _ = x.shape
    L = inputs["w_in"].shape[0]
    return run(inputs, S, L, B).astype(np.float32)
```

```python
import numpy as np
from contextlib import ExitStack
import concourse.bass as bass
import concourse.mybir as mybir
from concourse.bass_utils import run_bass_kernel_spmd

F32 = mybir.dt.float32
BF16 = mybir.dt.bfloat16
AF = mybir.ActivationFunctionType
ALU = mybir.AluOpType

D = 1024
T = 512
NH = 8
EPS = 1e-6
SEG = {}
_o = 0
for _n, _w in [("wina", 8 * 1216), ("wq", 3 * 1024), ("wk", 2 * 512), ("wv", 2 * 512), ("wbd", 6 * 128),
               ("wing", 8 * 3072), ("womla", 4 * 1024), ("wpp", 2 * 1024), ("wlp", 2 * 1024), ("wout", 8 * 1024),
               ("w1a", 8 * 2048), ("w2a", 16 * 1024), ("w1b", 8 * 2048), ("w2b", 16 * 1024)]:
    SEG[_n] = (_o, _w)
    _o += _w
WTOT = _o
NPL = 63


class Op:
    __slots__ = ("eng", "fn", "idx", "waits", "inc", "semval", "dma", "dsem", "dval", "clk", "dclk")


class _Rec:
    def __getattr__(self, name):
        def f(*a, **k):
            return (name, a, k)
        return f


_REC = _Rec()


def SK(key, sub):
    return ("§", key, sub)


class Sched:
    EPOCH = 30000
    SAME_ENG_WAR = True

    def __init__(self, nc, n_dma_sems=24):
        self.nc = nc
        self.h = {"pe": nc.tensor, "act": nc.scalar, "dve": nc.vector, "pool": nc.gpsimd, "sp": nc.sync}
        self.ops = {k: [] for k in self.h}
        self.lastw = {}
        self.readers = {}
        self.known = {k: {} for k in self.h}
        self.kdma = {k: {} for k in self.h}
        self.nds = n_dma_sems
        self.dcnt = [0] * n_dma_sems
        self.dlast = [None] * n_dma_sems
        self.drr = 0
        self.uid = 0
        self.npre = 0

    @staticmethod
    def _merge(a, b):
        for k, v in b.items():
            if a.get(k, -1) < v:
                a[k] = v

    @staticmethod
    def _ks(k):
        if isinstance(k, tuple) and len(k) == 3 and k[0] == "§":
            return k[1], k[2]
        return k, None

    def op(self, eng, fn, reads=(), writes=(), dma=False, pre=False):
        o = Op()
        o.eng = eng; o.fn = fn(_REC); o.idx = len(self.ops[eng]); o.dma = dma; o.inc = False; o.semval = 0
        o.dsem = None; o.dval = 0
        deps = []
        for k in reads:
            k, sb = self._ks(k)
            lw = self.lastw.get(k)
            if lw:
                if sb is None:
                    for w in lw.values():
                        deps.append((w, 0))
                else:
                    for s_ in (sb, None):
                        w = lw.get(s_)
                        if w is not None:
                            deps.append((w, 0))
        for k in writes:
            k, sb = self._ks(k)
            lw = self.lastw.get(k)
            if lw:
                if sb is None:
                    for w in lw.values():
                        deps.append((w, 1))
                else:
                    for s_ in (sb, None):
                        w = lw.get(s_)
                        if w is not None:
                            deps.append((w, 1))
            rd = self.readers.get(k)
            if rd:
                for s_, dd in rd.items():
                    if sb is None or s_ is None or s_ == sb:
                        for r in dd.values():
                            deps.append((r, 1))
        if dma and pre:
            o.dsem = ("pre", self.npre); o.dval = 16
            self.npre += 1
        elif dma:
            s = self.drr
            self.drr = (s + 1) % self.nds
            if self.dlast[s] is not None:
                deps.append((self.dlast[s], 1))
            self.dcnt[s] += 16
            o.dsem = s; o.dval = self.dcnt[s]; self.dlast[s] = o
        kn = self.known[eng]; kd = self.kdma[eng]
        waits = []
        for d, kind in deps:
            if d is o:
                continue
            if d.dma:
                if kd.get(d.dsem, 0) >= d.dval:
                    continue
                waits.append(d)
                kd[d.dsem] = d.dval
                self._merge(kn, d.clk); self._merge(kd, d.dclk)
            else:
                if d.eng == eng and (eng == "pe" or (kind == 1 and not self.SAME_ENG_WAR)):
                    continue
                if kn.get(d.eng, -1) >= d.idx:
                    continue
                waits.append(d)
                d.inc = True
                kn[d.eng] = d.idx
                self._merge(kn, d.clk); self._merge(kd, d.dclk)
        o.waits = waits
        o.clk = dict(kn); o.dclk = dict(kd)
        self.uid += 1
        for k in reads:
            k, sb = self._ks(k)
            self.readers.setdefault(k, {}).setdefault(sb, {})[("d", self.uid) if dma else eng] = o
        for k in writes:
            k, sb = self._ks(k)
            if sb is None:
                self.lastw[k] = {None: o}
                self.readers[k] = {}
            else:
                self.lastw.setdefault(k, {})[sb] = o
                rd = self.readers.get(k)
                if rd is not None:
                    rd[sb] = {}
        self.ops[eng].append(o)
        return o

    def barrier(self):
        lasts = []
        for e_, lst in self.ops.items():
            if e_ == "sp":
                continue
            for o_ in reversed(lst):
                if o_.fn is not None and not o_.dma:
                    lasts.append(o_)
                    break
        dl = [d for d in self.dlast if d is not None]
        for eng in self.h:
            kn = self.known[eng]; kd = self.kdma[eng]
            waits = []
            for d in lasts:
                if d.eng == eng or kn.get(d.eng, -1) >= d.idx:
                    continue
                waits.append(d); d.inc = True; kn[d.eng] = d.idx
            for d in dl:
                if kd.get(d.dsem, 0) >= d.dval:
                    continue
                waits.append(d); kd[d.dsem] = d.dval
            if waits:
                o = Op()
                o.eng = eng; o.fn = None; o.idx = len(self.ops[eng]); o.dma = False; o.inc = False; o.semval = 0
                o.dsem = None; o.dval = 0; o.waits = waits; o.clk = dict(kn); o.dclk = dict(kd)
                self.ops[eng].append(o)
        self.lastw = {k: v for k, v in self.lastw.items() if isinstance(k, tuple) and k[0] == "wbf"}
        self.readers = {}

    def emit(self, es):
        nc = self.nc
        E = self.EPOCH
        esems = {}
        for e, lst in self.ops.items():
            c = 0
            for o in lst:
                if o.inc and not o.dma and o.fn is not None:
                    c += 1
                    o.semval = c
            esems[e] = [es.enter_context(nc.semaphore(f"s_{e}_{i}")) for i in range(c // E + 1)]
        dsems = {i: es.enter_context(nc.semaphore(f"s_dma_{i}")) for i in range(self.nds)}
        for i in range(self.npre):
            dsems[("pre", i)] = es.enter_context(nc.semaphore(f"s_pre_{i}"))
        fin = es.enter_context(nc.semaphore("s_fin"))
        block = es.enter_context(nc.Block())
        ops = self.ops

        def run(e, name):
            lst = ops[name]
            for o in lst:
                for d in o.waits:
                    if d.dma:
                        e.wait_ge(dsems[d.dsem], d.dval)
                    else:
                        v = d.semval - 1
                        e.wait_ge(esems[d.eng][v // E], v % E + 1)
                if o.fn is None:
                    continue
                ins = getattr(e, o.fn[0])(*o.fn[1], **o.fn[2])
                if o.dma:
                    ins.then_inc(dsems[o.dsem], 16)
                elif o.inc:
                    v = o.semval - 1
                    ins.then_inc(esems[name][v // E], 1)

        @block.sync
        def _(e):
            run(e, "sp")

        @block.tensor
        def _(e):
            run(e, "pe")

        @block.scalar
        def _(e):
            run(e, "act")

        @block.vector
        def _(e):
            run(e, "dve")

        @block.gpsimd
        def _(e):
            run(e, "pool")


def build(S, L, dbg=False):
    _kw = dict(kind="ExternalOutput") if dbg else {}
    NB = S // T
    NT = S // 128
    nc = bass.Bass("TRN2", target_bir_lowering=False)
    xT = nc.dram_tensor("xT", [128, 8, S], F32, kind="ExternalInput").ap()
    wpk = nc.dram_tensor("wpk", [L, 128, WTOT], F32, kind="ExternalInput").ap()
    par_d = nc.dram_tensor("par", [128, NPL * L + 8], F32, kind="ExternalInput").ap()
    rope_d = nc.dram_tensor("rope", [128, 2, S], F32, kind="ExternalInput").ap()
    cst_d = nc.dram_tensor("cst", [128, 128], F32, kind="ExternalInput").ap()
    out_d = nc.dram_tensor("out", [128, 8, S], F32, kind="ExternalOutput").ap()
    xs = nc.dram_tensor("xs", [128, 8, S], F32, **_kw).ap()
    wbf = nc.dram_tensor("wbf", [L, 128, WTOT], BF16, **_kw).ap()
    ypre_s = nc.dram_tensor("ypre_s", [128, 2, S], BF16, **_kw).ap()
    hl_s = nc.dram_tensor("hl_s", [128, 2, S], BF16, **_kw).ap()
    QT_s = nc.dram_tensor("QT_s", [NH, 96, S], BF16, **_kw).ap()
    KT_s = nc.dram_tensor("KT_s", [NH, 64, S], BF16, **_kw).ap()
    kr_s = nc.dram_tensor("kr_s", [32, S], BF16, **_kw).ap()
    Vc_s = nc.dram_tensor("Vc_s", [128, NT, 4, 192], BF16, **_kw).ap()
    oT_s = nc.dram_tensor("oT_s", [4, 128, S], BF16, **_kw).ap()
    h2_s = nc.dram_tensor("h2_s", [128, 8, S], BF16, **_kw).ap()
    h1_s = nc.dram_tensor("h1_s", [128, 8, S], BF16, **_kw).ap()

    sc = Sched(nc)
    op = sc.op
    top = ExitStack()
    with top:
        _cnt = [0]

        def sbt(es, name, shape, dt):
            _cnt[0] += 1
            return es.enter_context(nc.sbuf_tensor(f"sb_{name}_{_cnt[0]}", shape, dt))

        ps_t = [top.enter_context(nc.psum_tensor(f"ps{i}", [128, T], F32)) for i in range(8)]
        psrr = [0]

        def ps():
            i = psrr[0]
            psrr[0] = (i + 1) % 8
            return ps_t[i], ("ps", i)

        NPAR = NPL * L + 8
        par = sbt(top, "par", [128, NPAR], F32)
        dpar = sbt(top, "dpar", [128, 32 * L], F32)
        cst = sbt(top, "cstf", [128, 128], F32)
        tri = sbt(top, "tri", [128, 128], BF16)
        ones = sbt(top, "ones", [128, 128], BF16)
        cb = sbt(top, "cb", [128, 4], F32)
        etmp = sbt(top, "etmp", [128, 2], F32)

        op("sp", lambda e: e.dma_start(out=par[:, :], in_=par_d[:, :]), [], ["par"], dma=True)
        op("sp", lambda e: e.dma_start(out=cst[:, :], in_=cst_d[:, :]), [], ["cst"], dma=True)
        op("dve", lambda e: e.tensor_copy(out=tri[:, :], in_=cst[:, :]), ["cst"], ["tri"])
        op("pool", lambda e: e.memset(ones[:, :], 1.0), [], ["ones"])
        op("pool", lambda e: e.memset(cb[:, 0:1], EPS), [], ["cb"])
        op("pool", lambda e: e.memset(cb[:, 1:2], 1.0), [], ["cb"])
        op("pool", lambda e: e.memset(cb[:, 2:3], -0.5), [], ["cb"])
        op("pool", lambda e: e.memset(cb[:, 3:4], 0.5), [], ["cb"])
        for l in range(L):
            b = l * NPL
            db = l * 32
            op("dve", lambda e, b=b, db=db: e.tensor_scalar(out=dpar[:, db:db + 4], in0=par[:, b + 25:b + 29], scalar1=0.5,
                                                           scalar2=None, op0=ALU.mult), ["par"], ["dpar"])
            op("dve", lambda e, b=b, db=db: e.tensor_scalar(out=dpar[:, db + 6:db + 30], in0=par[:, b + 31:b + 55], scalar1=0.5,
                                                           scalar2=None, op0=ALU.mult), ["par"], ["dpar"])
            op("act", lambda e, b=b: e.activation(out=etmp[:, :], in_=par[:, b + 29:b + 31], func=AF.Exp, scale=-1.0),
               ["par"], ["etmp"])
            op("act", lambda e: e.activation(out=etmp[:, :], in_=etmp[:, :], func=AF.Ln, bias=cb[:, 1:2], scale=1.0),
               ["etmp", "cb"], ["etmp"])
            op("dve", lambda e, db=db: e.tensor_scalar(out=dpar[:, db + 4:db + 6], in0=etmp[:, :], scalar1=-4.0,
                                                      scalar2=None, op0=ALU.mult), ["etmp"], ["dpar"])

        PSEGS = ["wina", "wq", "wk", "wv", "wbd", "wing", "womla", "wpp", "wlp", "wout", "w1a", "w2a", "w1b", "w2b"]

        def prepass_items(itemsl, after=()):
            for (l_, sname) in itemsl:
                o0, wdt = SEG[sname]
                op("pool", lambda e: e.dma_start(out=wbf[l_, :, o0:o0 + wdt], in_=wpk[l_, :, o0:o0 + wdt]),
                   list(after), [("wbf", l_, sname)], dma=True, pre=True)

        prepass_items([(0, n_) for n_ in PSEGS[0:5]])

        def load_w(tile3, l, sname, k, nch=1, only=None):
            o0, wdt = SEG[sname]
            n = wdt // k
            cw = n // nch
            src = wbf[l, :, o0:o0 + wdt].rearrange("p (k n) -> p k n", k=k)
            for i in range(nch):
                if only is not None and i not in only:
                    continue
                op("sp", lambda e: e.dma_start(out=tile3[:, :, i * cw:(i + 1) * cw], in_=src[:, :, i * cw:(i + 1) * cw]),
                   [("wbf", l, sname)], [(sname, i)], dma=True)

        def wkc(sname, c0, c1, cw):
            return [(sname, i) for i in range(c0 // cw, (c1 - 1) // cw + 1)]

        _dn = [0]

        def dump(name, tl, shape, dt, key):
            if not dbg:
                return
            _dn[0] += 1
            dd = nc.dram_tensor(f"dbg_{name}_{_dn[0]}", shape, dt, kind="ExternalOutput").ap()
            if len(shape) == 2:
                op("sp", lambda e: e.dma_start(out=dd[:, :], in_=tl[:, :]), [key], [], dma=True)
            else:
                op("sp", lambda e: e.dma_start(out=dd[:, :, :], in_=tl[:, :, :]), [key], [], dma=True)

        def rmsnorm(xb, xkey, nch, oi, gcol, sq, sqk, rs, rsk, rt, rtk, outt, outk, sq0=0):
            op("act", lambda e: e.activation(out=sq[:, sq0:sq0 + nch, :], in_=xb[:, 0:nch, :], func=AF.Square), [xkey], [sqk])
            pt, pk = ps()
            for c in range(nch):
                op("pe", lambda e, c=c: e.matmul(pt[:, :], lhsT=ones[:, :], rhs=sq[:, sq0 + c, :], start=(c == 0), stop=(c == nch - 1)),
                   [sqk, "ones"], [pk])
            op("act", lambda e: e.activation(out=rt[:, :], in_=pt[:, :], func=AF.Ln, bias=cb[:, 0:1], scale=oi), [pk, "cb"], [rtk])
            op("act", lambda e: e.activation(out=rs[:, :], in_=rt[:, :], func=AF.Exp, scale=-0.5), [rtk], [rsk])
            for c in range(nch):
                op("dve", lambda e, c=c: e.scalar_tensor_tensor(out=outt[:, c, :], in0=xb[:, c, :], scalar=par[:, gcol + c:gcol + c + 1],
                                                               in1=rs[:, :], op0=ALU.mult, op1=ALU.mult),
                   [xkey, rsk, "par"], [SK(outk, c)])

        def xsrc(l):
            return xT if l == 0 else xs

        for l in range(L):
            pb = l * NPL
            db = l * 32
            sc.barrier()
            with ExitStack() as es:
                wina = sbt(es, "wina", [128, 8, 1216], BF16)
                wq = sbt(es, "wq", [128, 3, 1024], BF16)
                wk = sbt(es, "wk", [128, 2, 512], BF16)
                wv = sbt(es, "wv", [128, 2, 512], BF16)
                wbd = sbt(es, "wbd", [128, 6, 128], BF16)
                xb = [sbt(es, f"xb{i}", [128, 8, T], F32) for i in range(2)]
                sq = sbt(es, "sq", [128, 8, T], BF16)
                hh_ = [sbt(es, f"h{i}", [128, 8, T], BF16) for i in range(2)]
                rsA = sbt(es, "rsA", [128, T], F32)
                rtA = sbt(es, "rtA", [128, T], F32)
                rsC = sbt(es, "rsC", [128, T], F32)
                rtC = sbt(es, "rtC", [128, T], F32)
                upool = sbt(es, "upool", [128, 2, 16 + T], F32)
                ulru = sbt(es, "ulru", [128, 2, 4 + T], F32)
                qlat = sbt(es, "qlat", [128, 3, T], F32)
                ckv = sbt(es, "ckv", [128, 2, T], F32)
                qn = sbt(es, "qn", [128, 3, T], BF16)
                ckvn = sbt(es, "ckvn", [128, 2, T], BF16)
                tab = [sbt(es, f"tab{i}", [128, 2, T], F32) for i in range(2)]
                t1 = sbt(es, "t1", [128, T], F32)
                t2 = sbt(es, "t2", [128, T], F32)
                kro = [sbt(es, f"kro{i}", [32, T], BF16) for i in range(2)]
                Kst = sbt(es, "Kst", [128, 4, T], BF16)
                Qn = sbt(es, "Qn", [128, 4, T], BF16)
                Qr = sbt(es, "Qr", [128, 2, T], BF16)
                Vst = sbt(es, "Vst", [128, 4, 4, 192], BF16)
                pa = sbt(es, "pa", [128, 2, 16 + T], F32)
                pbuf = sbt(es, "pbuf", [128, 2, 16 + T], F32)
                mixed = sbt(es, "mixed", [128, 2, T], BF16)
                ypre = sbt(es, "ypre", [128, 2, T], BF16)
                uc = sbt(es, "uc", [128, 2, T], F32)
                ucb = sbt(es, "ucb", [128, 2, T], BF16)
                tha = sbt(es, "tha", [128, 2, T], F32)
                thi = sbt(es, "thi", [128, 2, T], F32)
                bb = sbt(es, "bb", [128, 2, T], F32)
                hlf = sbt(es, "hlf", [128, 2, T], F32)
                hlb = sbt(es, "hlb", [128, 2, T], BF16)
                state = sbt(es, "state", [128, 2], F32)
                invw = sbt(es, "invw", [128, 2], F32)
                invc = sbt(es, "invc", [128, 2, 16], F32)

                op("sp", lambda e: e.dma_start(out=xb[0][:, :, :], in_=xsrc(l)[:, :, 0:T]), [("xs", 0)] if l > 0 else [], [("xb", 0)], dma=True)
                load_w(wina, l, "wina", 8, nch=4)
                op("sp", lambda e: e.dma_start(out=tab[0][:, :, :], in_=rope_d[:, :, 0:T]), [], [("tab", 0)], dma=True)
                load_w(wq, l, "wq", 3)
                load_w(wk, l, "wk", 2)
                load_w(wv, l, "wv", 2)
                load_w(wbd, l, "wbd", 6)
                op("pool", lambda e: e.memset(upool[:, :, 0:16], 0.0), [], ["upool"])
                op("pool", lambda e: e.memset(ulru[:, :, 0:4], 0.0), [], ["ulru"])
                op("pool", lambda e: e.memset(state[:, :], 0.0), [], ["state"])
                op("pool", lambda e: e.memset(Vst[:, :, :, 64:128], 1.0), [], ["Vst"])
                wins = [2, 4, 8, 16]
                for g in range(4):
                    p0, cc = (g % 2) * 64, g // 2
                    op("pool", lambda e: e.memset(invw[p0:p0 + 64, cc:cc + 1], 1.0 / wins[g]), [], ["invw"])
                    op("pool", lambda e: e.memset(invc[p0:p0 + 64, cc, :], 1.0 / wins[g]), [], ["invc"])
                    for t in range(wins[g] - 1):
                        op("pool", lambda e: e.memset(invc[p0:p0 + 64, cc, t:t + 1], 1.0 / (t + 1)), [], ["invc"])

                def load_x1(j):
                    src = xsrc(l)
                    op("sp", lambda e: e.dma_start(out=xb[j % 2][:, :, :], in_=src[:, :, j * T:(j + 1) * T]),
                       [("xs", j)] if l > 0 else [], [("xb", j % 2)], dma=True)

                def load_tab(j):
                    op("sp", lambda e: e.dma_start(out=tab[j % 2][:, :, :], in_=rope_d[:, :, j * T:(j + 1) * T]),
                       [], [("tab", j % 2)], dma=True)

                def stA(j):
                    i = j % 2
                    rmsnorm(xb[i], ("xb", i), 8, 1.0 / 1024, pb + 0, sq, "sq", rsA, "rsA", rtA, "rtA", hh_[i], ("h", i))
                    op("sp", lambda e: e.dma_start(out=h1_s[:, :, j * T:(j + 1) * T], in_=hh_[i][:, :, :]), [("h", i)], [("h1s", j)], dma=True)

                def stB(j):
                    i = j % 2
                    h = hh_[i]; hk = ("h", i)
                    TB = tab[i]; tk = ("tab", i)

                    def inproj(m0, mw):
                        pt, pk = ps()
                        for kc in range(8):
                            op("pe", lambda e: e.matmul(pt[0:mw, :], lhsT=wina[:, kc, m0:m0 + mw], rhs=h[:, kc, :],
                                                       start=(kc == 0), stop=(kc == 7)), [hk] + wkc("wina", m0, m0 + mw, 304), [pk])
                        return pt, pk

                    for c in range(3):
                        pt, pk = inproj(c * 128, 128)
                        op("act", lambda e: e.activation(out=qlat[:, c, :], in_=pt[:, :], func=AF.Copy), [pk], [SK("qlat", c)])
                    for c in range(2):
                        pt, pk = inproj(384 + c * 128, 128)
                        op("act", lambda e: e.activation(out=ckv[:, c, :], in_=pt[:, :], func=AF.Copy), [pk], [SK("ckv", c)])
                    pt, pk = inproj(640, 64)
                    op("dve", lambda e: e.tensor_tensor(out=t1[0:32, :], in0=pt[32:64, :], in1=TB[0:32, 1, :], op=ALU.mult), [pk, tk], ["t1"])
                    op("dve", lambda e: e.tensor_tensor(out=t2[0:32, :], in0=pt[0:32, :], in1=TB[0:32, 0, :], op=ALU.mult), [pk, tk], ["t2"])
                    op("dve", lambda e: e.tensor_tensor(out=kro[i][:, :], in0=t1[0:32, :], in1=t2[0:32, :], op=ALU.add), ["t1", "t2"], [("kro", i)])
                    op("sp", lambda e: e.dma_start(out=kr_s[:, j * T:(j + 1) * T], in_=kro[i][:, :]), [("kro", i)], [("krs", j)], dma=True)
                    for c in range(2):
                        pt, pk = inproj(704 + c * 128, 128)
                        op("act", lambda e: e.activation(out=upool[:, c, 16:16 + T], in_=pt[:, :], func=AF.Copy), [pk], [SK("upool", c)])
                    for c in range(2):
                        pt, pk = inproj(960 + c * 128, 128)
                        op("act", lambda e: e.activation(out=ulru[:, c, 4:4 + T], in_=pt[:, :], func=AF.Copy), [pk], [SK("ulru", c)])

                def stC(j):
                    rmsnorm(qlat, "qlat", 3, 1.0 / 384, pb + 10, sq, "sq", rsC, "rsC", rtC, "rtC", qn, "qn", sq0=0)
                    rmsnorm(ckv, "ckv", 2, 1.0 / 256, pb + 13, sq, "sq", rsC, "rsC", rtC, "rtC", ckvn, "ckvn", sq0=3)

                def stD(j):
                    i = j % 2
                    TB = tab[i]; tk = ("tab", i)
                    for p in range(4):
                        pt, pk = ps()
                        for kc in range(2):
                            op("pe", lambda e: e.matmul(pt[:, :], lhsT=wk[:, kc, p * 128:(p + 1) * 128], rhs=ckvn[:, kc, :],
                                                       start=(kc == 0), stop=(kc == 1)), ["ckvn", ("wk", 0)], [pk])
                        op("act", lambda e: e.activation(out=Kst[:, p, :], in_=pt[:, :], func=AF.Copy), [pk], [SK("Kst", p)])
                    ktv = KT_s.rearrange("(q e) p t -> e p q t", e=2)
                    for ev in range(2):
                        op("sp", lambda e: e.dma_start(out=ktv[ev, :, :, j * T:(j + 1) * T], in_=Kst[64 * ev:64 * ev + 64, :, :]),
                           ["Kst"], [("KT", j, ev)], dma=True)
                    for tt in range(4):
                        pt, pk = ps()
                        for kc in range(2):
                            op("pe", lambda e: e.matmul(pt[:, :], lhsT=ckvn[:, kc, tt * 128:(tt + 1) * 128], rhs=wv[:, kc, :],
                                                       start=(kc == 0), stop=(kc == 1)), ["ckvn", ("wv", 0)], [pk])
                        pv = pt[:, :].rearrange("p (q e c) -> p q e c", q=4, e=2)
                        op("act", lambda e: e.activation(out=Vst[:, tt, :, 0:64], in_=pv[:, :, 0, :], func=AF.Copy), [pk], [SK("Vst", (tt, 0))])
                        op("dve", lambda e: e.tensor_copy(out=Vst[:, tt, :, 128:192], in_=pv[:, :, 1, :]), [pk], [SK("Vst", (tt, 1))])
                    op("sp", lambda e: e.dma_start(out=Vc_s[:, 4 * j:4 * j + 4, :, :], in_=Vst[:, :, :, :]), ["Vst"], [("Vc", j)], dma=True)
                    def qmm(c0):
                        pt, pk = ps()
                        for kc in range(3):
                            op("pe", lambda e: e.matmul(pt[:, :], lhsT=wq[:, kc, c0:c0 + 128], rhs=qn[:, kc, :],
                                                       start=(kc == 0), stop=(kc == 2)), ["qn", ("wq", 0)], [pk])
                        return pt, pk
                    for g in range(2):
                        ptr, pkr = qmm(512 + g * 128)
                        ptp, pkp = qmm(768 + g * 128)
                        op("dve", lambda e: e.tensor_tensor(out=t1[:, :], in0=ptp[:, :], in1=TB[:, 1, :], op=ALU.mult), [pkp, tk], ["t1"])
                        op("dve", lambda e: e.tensor_tensor(out=t2[:, :], in0=ptr[:, :], in1=TB[:, 0, :], op=ALU.mult), [pkr, tk], ["t2"])
                        op("dve", lambda e: e.tensor_tensor(out=Qr[:, g, :], in0=t1[:, :], in1=t2[:, :], op=ALU.add), ["t1", "t2"], [SK("Qr", g)])
                    for p in range(4):
                        pt, pk = qmm(p * 128)
                        op("act", lambda e: e.activation(out=Qn[:, p, :], in_=pt[:, :], func=AF.Copy), [pk], [SK("Qn", p)])
                    qtv = QT_s.rearrange("(q e) p t -> e p q t", e=2)
                    for ev in range(2):
                        op("sp", lambda e: e.dma_start(out=qtv[ev, 0:64, :, j * T:(j + 1) * T], in_=Qn[64 * ev:64 * ev + 64, :, :]),
                           ["Qn"], [("QT", j, ev)], dma=True)
                    qrv = QT_s.rearrange("(g hq) p t -> hq p g t", hq=4)
                    for hq in range(4):
                        op("sp", lambda e: e.dma_start(out=qrv[hq, 64:96, :, j * T:(j + 1) * T], in_=Qr[32 * hq:32 * hq + 32, :, :]),
                           ["Qr"], [("QT", j, 2 + hq)], dma=True)

                def stE1(j):
                    W = 16 + T
                    op("pool", lambda e: e.tensor_tensor(out=pa[:, :, 1:W], in0=upool[:, :, 1:W], in1=upool[:, :, 0:W - 1], op=ALU.add), ["upool"], ["pa"])
                    op("pool", lambda e: e.tensor_tensor(out=pbuf[:, :, 3:W], in0=pa[:, :, 3:W], in1=pa[:, :, 1:W - 2], op=ALU.add), ["pa"], ["pbuf"])
                    op("pool", lambda e: e.tensor_tensor(out=pa[:, 1, 7:W], in0=pbuf[:, 1, 7:W], in1=pbuf[:, 1, 3:W - 4], op=ALU.add), ["pbuf"], ["pa"])
                    op("pool", lambda e: e.tensor_tensor(out=pbuf[64:128, 1, 16:W], in0=pa[64:128, 1, 16:W], in1=pa[64:128, 1, 8:W - 8], op=ALU.add), ["pa", "pbuf"], ["pbuf"])
                    for g in range(4):
                        p0, c = (g % 2) * 64, g // 2
                        srcb = pa if g % 2 == 0 else pbuf
                        op("dve", lambda e: e.scalar_tensor_tensor(out=mixed[p0:p0 + 64, c, :], in0=srcb[p0:p0 + 64, c, 16:W], scalar=invw[p0:p0 + 64, c:c + 1],
                                                                  in1=upool[p0:p0 + 64, c, 16:W], op0=ALU.mult, op1=ALU.subtract),
                           ["pa", "pbuf", "upool", "invw"], [SK("mixed", g)])
                        if j == 0:
                            op("dve", lambda e: e.tensor_tensor(out=t1[p0:p0 + 64, 0:16], in0=srcb[p0:p0 + 64, c, 16:32], in1=invc[p0:p0 + 64, c, :], op=ALU.mult),
                               ["pa", "pbuf", "invc"], ["t1"])
                            op("dve", lambda e: e.tensor_tensor(out=mixed[p0:p0 + 64, c, 0:16], in0=t1[p0:p0 + 64, 0:16], in1=upool[p0:p0 + 64, c, 16:32], op=ALU.subtract),
                               ["t1", "upool"], [SK("mixed", g)])
                    op("pool", lambda e: e.tensor_copy(out=upool[:, :, 0:16], in_=upool[:, :, T:T + 16]), ["upool", "mixed"], ["upool"])
                    cw = pb + 15
                    for c in range(2):
                        op("dve", lambda e: e.tensor_scalar(out=uc[:, c, :], in0=ulru[:, c, 1:1 + T], scalar1=par[:, cw + c:cw + c + 1],
                                                           scalar2=par[:, pb + 23 + c:pb + 24 + c], op0=ALU.mult, op1=ALU.add),
                           ["ulru", "par"], [SK("uc", c)])
                        for k in range(1, 4):
                            op("dve", lambda e: e.scalar_tensor_tensor(out=uc[:, c, :], in0=ulru[:, c, 1 + k:1 + k + T],
                                                                      scalar=par[:, cw + 2 * k + c:cw + 2 * k + c + 1],
                                                                      in1=uc[:, c, :], op0=ALU.mult, op1=ALU.add),
                               ["ulru", "par", SK("uc", c)], [SK("uc", c)])
                    op("act", lambda e: e.activation(out=ucb[:, :, :], in_=uc[:, :, :], func=AF.Copy), ["uc"], ["ucb"])
                    op("pool", lambda e: e.tensor_copy(out=ulru[:, :, 0:4], in_=ulru[:, :, T:T + 4]), ["ulru", "uc"], ["ulru"])

                def stE2(j):
                    for c in range(2):
                        pt, pk = ps()
                        op("pe", lambda e: e.matmul(pt[:, :], lhsT=wbd[:, c, :], rhs=mixed[:, c, :], start=True, stop=True), ["mixed", ("wbd", 0)], [pk])
                        op("dve", lambda e: e.tensor_scalar(out=ypre[:, c, :], in0=pt[:, :], scalar1=par[:, pb + 8 + c:pb + 9 + c], scalar2=None,
                                                           op0=ALU.mult), [pk, "par"], [SK("ypre", c)])
                    op("sp", lambda e: e.dma_start(out=ypre_s[:, :, j * T:(j + 1) * T], in_=ypre[:, :, :]), ["ypre"], [("ypre", j)], dma=True)
                    for c in range(2):
                        pt, pk = ps()
                        op("pe", lambda e: e.matmul(pt[:, :], lhsT=wbd[:, 2 + c, :], rhs=ucb[:, c, :], start=True, stop=True), ["ucb", ("wbd", 0)], [pk])
                        op("act", lambda e: e.activation(out=tha[:, c, :], in_=pt[:, :], func=AF.Tanh, bias=dpar[:, db + c:db + c + 1], scale=0.5),
                           [pk, "dpar"], [SK("tha", c)])
                        pt2, pk2 = ps()
                        op("pe", lambda e: e.matmul(pt2[:, :], lhsT=wbd[:, 4 + c, :], rhs=ucb[:, c, :], start=True, stop=True), ["ucb", ("wbd", 0)], [pk2])
                        op("act", lambda e: e.activation(out=thi[:, c, :], in_=pt2[:, :], func=AF.Tanh, bias=dpar[:, db + 2 + c:db + 3 + c], scale=0.5),
                           [pk2, "dpar"], [SK("thi", c)])
                        op("act", lambda e: e.activation(out=tha[:, c, :], in_=tha[:, c, :], func=AF.Exp, bias=dpar[:, db + 4 + c:db + 5 + c],
                                                        scale=dpar[:, db + 4 + c:db + 5 + c]), [SK("tha", c), "dpar"], [SK("tha", c)])
                    op("dve", lambda e: e.scalar_tensor_tensor(out=bb[:, :, :], in0=tha[:, :, :], scalar=0.99999997, in1=tha[:, :, :],
                                                              op0=ALU.min, op1=ALU.mult), ["tha"], ["bb"])
                    op("act", lambda e: e.activation(out=bb[:, :, :], in_=bb[:, :, :], func=AF.Ln, bias=cb[:, 1:2], scale=-1.0), ["bb", "cb"], ["bb"])
                    op("act", lambda e: e.activation(out=bb[:, :, :], in_=bb[:, :, :], func=AF.Exp, scale=0.5), ["bb"], ["bb"])
                    op("act", lambda e: e.activation(out=thi[:, :, :], in_=thi[:, :, :], func=AF.Identity, bias=cb[:, 3:4], scale=0.5), ["thi", "cb"], ["thi"])

                def stE2b(j):
                    op("pool", lambda e: e.tensor_tensor(out=thi[:, :, :], in0=thi[:, :, :], in1=uc[:, :, :], op=ALU.mult), ["thi", "uc"], ["thi"])
                    op("dve", lambda e: e.tensor_tensor(out=bb[:, :, :], in0=bb[:, :, :], in1=thi[:, :, :], op=ALU.mult), ["bb", "thi"], ["bb"])
                    for c in range(2):
                        op("dve", lambda e: e.tensor_tensor_scan(out=hlf[:, c, :], data0=tha[:, c, :], data1=bb[:, c, :], initial=state[:, c:c + 1],
                                                                op0=ALU.mult, op1=ALU.add), ["tha", "bb", "state"], [SK("hlf", c)])
                    op("dve", lambda e: e.tensor_copy(out=state[:, :], in_=hlf[:, :, T - 1]), ["hlf"], ["state"])
                    op("act", lambda e: e.activation(out=hlb[:, :, :], in_=hlf[:, :, :], func=AF.Copy), ["hlf"], ["hlb"])
                    op("sp", lambda e: e.dma_start(out=hl_s[:, :, j * T:(j + 1) * T], in_=hlb[:, :, :]), ["hlb"], [("hl", j)], dma=True)

                if NB > 1:
                    load_x1(1)
                    load_tab(1)
                stA(0)
                stB(0)
                if NB > 1:
                    stA(1)
                for j in range(NB):
                    if j + 2 < NB:
                        load_x1(j + 2)
                    stC(j)
                    stE1(j)
                    if j + 1 < NB:
                        stB(j + 1)
                    if j + 2 < NB:
                        stA(j + 2)
                    stE2(j)
                    stD(j)
                    if j + 2 < NB:
                        load_tab(j + 2)
                    stE2b(j)

            sc.barrier()
            with ExitStack() as es:
                Kh = [sbt(es, f"Kh{i}", [96, S], BF16) for i in range(2)]
                Qh = [sbt(es, f"Qh{i}", [96, S], BF16) for i in range(2)]
                Vp = [sbt(es, f"Vp{i}", [128, NT, 192], BF16) for i in range(2)]
                oT = [sbt(es, f"oT{i}", [128, S], BF16) for i in range(2)]
                NP_ = 6
                Pt = [sbt(es, f"P{i}", [128, T], BF16) for i in range(NP_)]
                rec = [sbt(es, f"rec{i}", [128, T], F32) for i in range(2)]
                allj = list(range(NB))
                scale = 96.0 ** -0.5
                LA = 3
                pend = ([(0, n_) for n_ in PSEGS[5:]] if l == 0 else []) + ([(l + 1, n_) for n_ in PSEGS] if l + 1 < L else [])
                pchunks = [pend[i_:i_ + 7] for i_ in range(0, len(pend), 7)]
                items = [(hh, j, kt) for hh in range(NH) for j in range(NB) for kt in range(4 * j + 4)]
                NI = len(items)

                def bufs(hh):
                    p = hh // 2
                    return (Kh[hh % 2], Qh[hh % 2], Vp[p % 2], oT[p % 2], ("Kh", hh % 2), ("Qh", hh % 2), ("Vp", p % 2), ("oT", p % 2))

                def load_head(h2):
                    K2, Q2, V2, O2, kk2, qk2, vk2, ok2 = bufs(h2)
                    op("sp", lambda e: e.dma_start(out=K2[0:64, :], in_=KT_s[h2, :, :]), [("KT", jj, ev_) for jj in allj for ev_ in range(2)], [SK(kk2, 0)], dma=True)
                    op("sp", lambda e: e.dma_start(out=K2[64:96, :], in_=kr_s[:, :]), [("krs", jj) for jj in allj], [SK(kk2, 1)], dma=True)
                    op("sp", lambda e: e.dma_start(out=Q2[:, :], in_=QT_s[h2, :, :]), [("QT", jj, x_) for jj in allj for x_ in range(6)], [qk2], dma=True)

                def load_v(p2):
                    V2 = Vp[p2 % 2]
                    op("sp", lambda e: e.dma_start(out=V2[:, :, :], in_=Vc_s[:, :, p2, :]), [("Vc", jj) for jj in allj], [("Vp", p2 % 2)], dma=True)

                def qk(ii):
                    hh, j, kt = items[ii]
                    p, ev = hh // 2, hh % 2
                    K_, Q_, V_, O_, kk, qk_, vk, ok = bufs(hh)
                    if j == 0 and kt == 0:
                        if hh == 0:
                            load_head(0)
                            load_v(0)
                        if hh + 1 < NH:
                            load_head(hh + 1)
                        if hh % 2 == 0 and (hh // 2) < len(pchunks):
                            prepass_items(pchunks[hh // 2], after=[("Kh", 0), ("Qh", 0), ("Vp", p % 2), ("Kh", 1), ("Qh", 1)])
                        if ev == 1 and p + 1 < 4:
                            load_v(p + 1)
                    dd = kt - 4 * j
                    c0 = 128 * dd if dd >= 0 else 0
                    st = ps_t[ii % 4]; sk = ("ps", ii % 4)
                    Pm = Pt[ii % NP_]; pk_ = ("P", ii % NP_)
                    op("pe", lambda e: e.matmul(st[:, c0:T], lhsT=K_[:, kt * 128:(kt + 1) * 128], rhs=Q_[:, j * T + c0:(j + 1) * T], start=True, stop=True),
                       [kk, qk_], [sk])
                    op("act", lambda e: e.activation(out=Pm[:, c0:T], in_=st[:, c0:T], func=AF.Exp, scale=scale), [sk], [pk_])
                    if dd >= 0:
                        op("dve", lambda e: e.tensor_tensor(out=Pm[:, c0:c0 + 128], in0=Pm[:, c0:c0 + 128], in1=tri[:, :], op=ALU.mult),
                           [pk_, "tri"], [pk_])

                def pv(ii):
                    hh, j, kt = items[ii]
                    p, ev = hh // 2, hh % 2
                    K_, Q_, V_, O_, kk, qk_, vk, ok = bufs(hh)
                    dd = kt - 4 * j
                    c0 = 128 * dd if dd >= 0 else 0
                    nk = 4 * j + 4
                    Pm = Pt[ii % NP_]; pk_ = ("P", ii % NP_)
                    acc = ps_t[4 + (j % 4)]; ak = ("ps", 4 + (j % 4))
                    op("pe", lambda e: e.matmul(acc[:, c0:T], lhsT=V_[:, kt, ev * 64:ev * 64 + 128], rhs=Pm[:, c0:T], start=(kt == 0), stop=(kt == nk - 1)),
                       [vk, pk_], [ak])
                    if kt == nk - 1:
                        vb, dbs = (0, 64) if ev == 0 else (64, 0)
                        R = rec[j % 2]; rk = ("rec", j % 2)
                        op("dve", lambda e: e.reciprocal(out=R[dbs:dbs + 64, :], in_=acc[dbs:dbs + 64, :]), [ak], [rk])
                        op("dve", lambda e: e.tensor_tensor(out=O_[vb:vb + 64, j * T:(j + 1) * T], in0=acc[vb:vb + 64, :], in1=R[dbs:dbs + 64, :], op=ALU.mult),
                           [ak, rk], [SK(ok, (ev, j))])
                        if j == NB - 1 and ev == 1:
                            op("sp", lambda e: e.dma_start(out=oT_s[p, :, :], in_=O_[:, :]), [ok], [("oTs", p)], dma=True)

                for ii in range(NI + LA):
                    if ii < NI:
                        qk(ii)
                    if ii - LA >= 0:
                        pv(ii - LA)

            sc.barrier()
            with ExitStack() as es:
                wing = sbt(es, "wing", [128, 8, 3072], BF16)
                womla = sbt(es, "womla", [128, 4, 1024], BF16)
                wpp = sbt(es, "wpp", [128, 2, 1024], BF16)
                wlp = sbt(es, "wlp", [128, 2, 1024], BF16)
                wout = sbt(es, "wout", [128, 8, 1024], BF16)
                xb = [sbt(es, f"xb3_{i}", [128, 8, T], F32) for i in range(2)]
                ypb = [sbt(es, f"ypb{i}", [128, 2, T], BF16) for i in range(2)]
                hlb3 = [sbt(es, f"hlb3_{i}", [128, 2, T], BF16) for i in range(2)]
                otb = [sbt(es, f"otb{i}", [128, 4, T], BF16) for i in range(2)]
                hb = [sbt(es, f"h3_{i}", [128, 8, T], BF16) for i in range(2)]
                th = [sbt(es, f"th{i}", [128, T], BF16) for i in range(6)]
                mt = [sbt(es, f"mt{i}", [128, T], F32) for i in range(6)]
                merged = sbt(es, "merged", [128, 8, T], BF16)

                def load_x3(j, first=False):
                    src = xsrc(l)
                    i = j % 2
                    op("sp", lambda e: e.dma_start(out=hb[i][:, :, :], in_=h1_s[:, :, j * T:(j + 1) * T]), [("h1s", j)], [("hb", i)], dma=True)
                    if first:
                        load_w(wing, l, "wing", 8, nch=4, only=[0])
                        load_w(wpp, l, "wpp", 2)
                    op("sp", lambda e: e.dma_start(out=ypb[i][:, :, :], in_=ypre_s[:, :, j * T:(j + 1) * T]), [("ypre", j)], [("ypb", i)], dma=True)
                    if first:
                        load_w(womla, l, "womla", 4)
                    op("sp", lambda e: e.dma_start(out=otb[i][:, :, :], in_=oT_s[:, :, j * T:(j + 1) * T].rearrange("q p t -> p q t")),
                       [("oTs", p) for p in range(4)], [("otb", i)], dma=True)
                    if first:
                        load_w(wlp, l, "wlp", 2)
                    op("sp", lambda e: e.dma_start(out=hlb3[i][:, :, :], in_=hl_s[:, :, j * T:(j + 1) * T]), [("hl", j)], [("hlb3", i)], dma=True)
                    op("sp", lambda e: e.dma_start(out=xb[i][:, :, :], in_=src[:, :, j * T:(j + 1) * T]),
                       [("xs", j)] if l > 0 else [], [("xb", i)], dma=True)

                load_x3(0, first=True)
                load_w(wing, l, "wing", 8, nch=4, only=[1, 2, 3])
                load_w(wout, l, "wout", 8)
                trr = 0
                for j in range(NB):
                    if j + 1 < NB:
                        load_x3(j + 1)
                    i = j % 2
                    X = xb[i]; xk = ("xb", i)
                    h = hb[i]; hkey = ("hb", i)
                    for c in range(8):
                        srcs = [(wpp, ("wpp", 0), ypb[i], ("ypb", i), 2), (womla, ("womla", 0), otb[i], ("otb", i), 4), (wlp, ("wlp", 0), hlb3[i], ("hlb3", i), 2)]
                        mts = []
                        for b in range(3):
                            TH = th[trr % 6]; thk = ("th", trr % 6)
                            MT = mt[trr % 6]; mtk = ("mt", trr % 6)
                            trr += 1
                            pt, pk = ps()
                            for kc in range(8):
                                op("pe", lambda e, kc=kc, b=b, c=c, pt=pt: e.matmul(pt[:, :], lhsT=wing[:, kc, (c * 3 + b) * 128:(c * 3 + b + 1) * 128], rhs=h[:, kc, :],
                                                                                 start=(kc == 0), stop=(kc == 7)), [hkey, ("wing", c // 2)], [pk])
                            gc = db + 6 + b * 8 + c
                            op("act", lambda e, pt=pt, TH=TH, gc=gc: e.activation(out=TH[:, :], in_=pt[:, :], func=AF.Tanh, bias=dpar[:, gc:gc + 1], scale=0.5),
                               [pk, "dpar"], [thk])
                            wt_, wkey, src, skey, nk = srcs[b]
                            pt2, pk2 = ps()
                            for kc in range(nk):
                                op("pe", lambda e, kc=kc, c=c, pt2=pt2, wt_=wt_, src=src, nk=nk: e.matmul(pt2[:, :], lhsT=wt_[:, kc, c * 128:(c + 1) * 128], rhs=src[:, kc, :],
                                                                                                     start=(kc == 0), stop=(kc == nk - 1)), [skey, wkey], [pk2])
                            op("dve", lambda e, TH=TH, MT=MT, pt2=pt2: e.scalar_tensor_tensor(out=MT[:, :], in0=TH[:, :], scalar=1.0, in1=pt2[:, :],
                                                                                           op0=ALU.add, op1=ALU.mult), [thk, pk2], [mtk])
                            mts.append((MT, mtk))
                        op("pool", lambda e, a=mts[0][0], b_=mts[1][0]: e.tensor_tensor(out=a[:, :], in0=a[:, :], in1=b_[:, :], op=ALU.add),
                           [mts[0][1], mts[1][1]], [mts[0][1]])
                        op("pool", lambda e, a=mts[0][0], b_=mts[2][0], c=c: e.tensor_tensor(out=merged[:, c, :], in0=a[:, :], in1=b_[:, :], op=ALU.add),
                           [mts[0][1], mts[2][1]], [("merged", c)])
                    for m in range(8):
                        pt, pk = ps()
                        for kc in range(8):
                            op("pe", lambda e, kc=kc, m=m, pt=pt: e.matmul(pt[:, :], lhsT=wout[:, kc, m * 128:(m + 1) * 128], rhs=merged[:, kc, :],
                                                                        start=(kc == 0), stop=(kc == 7)), [("merged", kc), ("wout", 0)], [pk])
                        op("dve", lambda e, m=m, pt=pt, X=X: e.scalar_tensor_tensor(out=X[:, m, :], in0=pt[:, :], scalar=0.5, in1=X[:, m, :],
                                                                                 op0=ALU.mult, op1=ALU.add), [pk, SK(xk, m)], [SK(xk, m)])
                    op("sp", lambda e, j=j, X=X: e.dma_start(out=xs[:, :, j * T:(j + 1) * T], in_=X[:, :, :]), [xk], [("xs", j)], dma=True)

            for half in range(2):
                sc.barrier()
                last = (l == L - 1 and half == 1)
                with ExitStack() as es:
                    w1 = sbt(es, "w1", [128, 8, 2048], BF16)
                    w2 = sbt(es, "w2", [128, 16, 1024], BF16)
                    xb = [sbt(es, f"xb4_{i}", [128, 8, T], F32) for i in range(2)]
                    h2 = [sbt(es, f"h2_{i}", [128, 8, T], BF16) for i in range(2)]
                    sq = sbt(es, "sq4", [128, 8, T], BF16)
                    rs = sbt(es, "rs4", [128, T], F32)
                    rt = sbt(es, "rt4", [128, T], F32)
                    hid = sbt(es, "hid", [128, 16, T], BF16)
                    rl = [sbt(es, f"rl{i}", [128, T], F32) for i in range(3)]
                    ob = [sbt(es, f"ob{i}", [128, 8, T], F32) for i in range(2)] if last else None
                    w1n = "w1a" if half == 0 else "w1b"
                    w2n = "w2a" if half == 0 else "w2b"

                    def load_x4(j, first=False):
                        i = j % 2
                        if half == 1:
                            op("sp", lambda e: e.dma_start(out=h2[i][:, :, :], in_=h2_s[:, :, j * T:(j + 1) * T]), [("h2s", j)], [("h2", i)], dma=True)
                            if first:
                                load_w(w1, l, w1n, 8, nch=4, only=[0])
                        op("sp", lambda e: e.dma_start(out=xb[i][:, :, :], in_=xs[:, :, j * T:(j + 1) * T]), [("xs", j)], [("xb", i)], dma=True)
                        if half == 0 and first:
                            load_w(w1, l, w1n, 8, nch=4, only=[0])

                    load_x4(0, first=True)
                    load_w(w1, l, w1n, 8, nch=4, only=[1, 2, 3])
                    load_w(w2, l, w2n, 16)

                    def norm4(j):
                        i = j % 2
                        rmsnorm(xb[i], ("xb", i), 8, 1.0 / 1024, pb + 55, sq, "sq", rs, "rs", rt, "rt", h2[i], ("h2", i))
                        op("sp", lambda e: e.dma_start(out=h2_s[:, :, j * T:(j + 1) * T], in_=h2[i][:, :, :]), [("h2", i)], [("h2s", j)], dma=True)

                    if half == 0:
                        norm4(0)
                    rrr = 0
                    for j in range(NB):
                        if j + 1 < NB:
                            load_x4(j + 1)
                        i = j % 2
                        X = xb[i]; xk = ("xb", i)
                        H = h2[i]; hk = ("h2", i)
                        for f in range(16):
                            pt, pk = ps()
                            for kc in range(8):
                                op("pe", lambda e, kc=kc, f=f, pt=pt, H=H: e.matmul(pt[:, :], lhsT=w1[:, kc, f * 128:(f + 1) * 128], rhs=H[:, kc, :],
                                                                                 start=(kc == 0), stop=(kc == 7)), [hk, (w1n, f // 4)], [pk])
                            RL = rl[rrr % 3]; rlk = ("rl", rrr % 3); rrr += 1
                            op("act", lambda e, pt=pt, RL=RL: e.activation(out=RL[:, :], in_=pt[:, :], func=AF.Relu), [pk], [rlk])
                            op("dve" if f % 2 == 0 else "pool", lambda e, RL=RL, f=f: e.tensor_tensor(out=hid[:, f, :], in0=RL[:, :], in1=RL[:, :], op=ALU.mult),
                               [rlk], [("hid", f)])
                        if half == 0 and j + 1 < NB:
                            norm4(j + 1)
                        for m in range(8):
                            pt, pk = ps()
                            for kc in range(16):
                                op("pe", lambda e, kc=kc, m=m, pt=pt: e.matmul(pt[:, :], lhsT=w2[:, kc, m * 128:(m + 1) * 128], rhs=hid[:, kc, :],
                                                                            start=(kc == 0), stop=(kc == 15)), [("hid", kc), (w2n, 0)], [pk])
                            op("dve", lambda e, m=m, pt=pt, X=X: e.tensor_tensor(out=X[:, m, :], in0=pt[:, :], in1=X[:, m, :], op=ALU.add), [pk, SK(xk, m)], [SK(xk, m)])
                        if not last:
                            op("sp", lambda e, j=j, X=X: e.dma_start(out=xs[:, :, j * T:(j + 1) * T], in_=X[:, :, :]), [xk], [("xs", j)], dma=True)
                        else:
                            O = ob[i]; okk = ("ob", i)
                            rmsnorm(X, xk, 8, 1.0 / 1024, NPL * L, sq, "sq", rs, "rs", rt, "rt", O, okk)
                            op("sp", lambda e, j=j, O=O: e.dma_start(out=out_d[:, :, j * T:(j + 1) * T], in_=O[:, :, :]), [okk], [("out", j)], dma=True)

        sc.barrier()
        sc.emit(top)
    return nc


def _pk(Wm):
    K, N = Wm.shape
    kc = K // 128
    return np.ascontiguousarray(Wm.reshape(kc, 128, N).transpose(1, 0, 2).reshape(128, kc * N))


def _bd(w4):
    o = np.zeros((128, 2, 128), np.float32)
    for g in range(4):
        p0, c = (g % 2) * 64, g // 2
        o[p0:p0 + 64, c, p0:p0 + 64] = w4[g]
    return o.reshape(128, 256)


def pack_host(inp, S, L):
    f = lambda a: np.asarray(a, dtype=np.float32)
    wpk = np.zeros((L, 128, WTOT), np.float32)
    par = np.zeros((128, NPL * L + 8), np.float32)
    perm = np.concatenate([np.arange(16, 32), np.arange(0, 16)])
    for l in range(L):
        w_in = f(inp["w_in"][l])
        kr = w_in[:, 896:928]
        wina = np.concatenate([w_in[:, 256:640], w_in[:, 640:896], kr, kr[:, perm], w_in[:, 0:256], w_in[:, 928:1184]], axis=1)
        assert wina.shape[1] == 1216
        wq0 = f(inp["w_q_up"][l]).reshape(384, 8, 96)
        wq = np.concatenate([wq0[:, :, 0:64].reshape(384, 512), wq0[:, :, 64:96].reshape(384, 256),
                             wq0[:, :, 64 + perm].reshape(384, 256)], axis=1)
        wkv = f(inp["w_kv_up"][l]).reshape(256, 8, 128)
        wk = np.ascontiguousarray(wkv[:, :, 0:64]).reshape(256, 512)
        wv = np.ascontiguousarray(wkv[:, :, 64:128]).reshape(256, 512)
        wbd = np.concatenate([_bd(f(inp["w_pool_grp"][l])), _bd(f(inp["w_lru_a"][l])), _bd(f(inp["w_lru_x"][l]))], axis=1)
        w1 = f(inp["w_ff1"][l]); w2 = f(inp["w_ff2"][l])
        segs = {"wina": _pk(wina), "wq": _pk(wq), "wk": _pk(wk), "wv": _pk(wv), "wbd": wbd,
                "wing": _pk(np.ascontiguousarray(w_in[:, 1184:].reshape(1024, 3, 8, 128).transpose(0, 2, 1, 3)).reshape(1024, 3072)), "womla": _pk(f(inp["w_mla_o"][l])), "wpp": _pk(f(inp["w_pool_proj"][l])),
                "wlp": _pk(f(inp["w_lru_proj"][l])), "wout": _pk(f(inp["w_out"][l])),
                "w1a": _pk(w1[:, 0:2048]), "w2a": _pk(w2[0:2048, :]), "w1b": _pk(w1[:, 2048:]), "w2b": _pk(w2[2048:, :])}
        for n, a in segs.items():
            o0, wdt = SEG[n]
            assert a.shape == (128, wdt), (n, a.shape, wdt)
            wpk[l, :, o0:o0 + wdt] = a
        b = l * NPL
        col = lambda v: f(v).reshape(-1, 128).T
        par[:, b + 0:b + 8] = col(inp["g_mix"][l])
        par[:, b + 8:b + 10] = col(inp["pool_scale"][l])
        par[:, b + 10:b + 13] = col(inp["g_q"][l])
        par[:, b + 13:b + 15] = col(inp["g_kv"][l])
        par[:, b + 15:b + 23] = col(inp["conv_w"][l])
        par[:, b + 23:b + 25] = col(inp["conv_b"][l])
        par[:, b + 25:b + 27] = col(inp["b_lru_a"][l])
        par[:, b + 27:b + 29] = col(inp["b_lru_x"][l])
        par[:, b + 29:b + 31] = col(inp["lru_lambda"][l])
        par[:, b + 31:b + 55] = col(inp["b_gate"][l])
        par[:, b + 55:b + 63] = col(inp["g_ffn"][l])
    par[:, NPL * L:NPL * L + 8] = f(inp["g_final"]).reshape(-1, 128).T
    pos = np.arange(S, dtype=np.float32)
    inv = (np.float32(10000.0) ** (-np.arange(0, 32, 2, dtype=np.float32) / np.float32(32))).astype(np.float32)
    ang = (pos[:, None] * inv[None, :]).astype(np.float32)
    c = np.cos(ang).T.astype(np.float32); s = np.sin(ang).T.astype(np.float32)
    rope = np.zeros((128, 2, S), np.float32)
    for r_ in range(4):
        rope[32 * r_:32 * r_ + 16, 0] = c; rope[32 * r_ + 16:32 * r_ + 32, 0] = c
        rope[32 * r_:32 * r_ + 16, 1] = -s; rope[32 * r_ + 16:32 * r_ + 32, 1] = s
    tri = (np.arange(128)[None, :] >= np.arange(128)[:, None]).astype(np.float32)
    return wpk, par, rope, tri


_NC_CACHE = {}


def run(inp, S, L, B):
    wpk, par, rope, tri = pack_host(inp, S, L)
    x = np.asarray(inp["x"], dtype=np.float32)
    key = (S, L)
    if key not in _NC_CACHE:
        _NC_CACHE[key] = build(S, L)
    nc = _NC_CACHE[key]
    in_maps = []
    for b in range(B):
        xTb = np.ascontiguousarray(x[b].reshape(S, 8, 128).transpose(2, 1, 0))
        in_maps.append({"xT": xTb, "wpk": wpk, "par": par, "rope": rope, "cst": tri})
    res = run_bass_kernel_spmd(nc, in_maps, core_ids=list(range(B)))
    outs = []
    for b in range(B):
        o = np.asarray(res.results[b]["out"], dtype=np.float32)
        outs.append(o.transpose(2, 1, 0).reshape(S, D))
    return np.stack(outs, axis=0)


def kernel(**inputs):
    x = inputs["x"]
    B, S, _ = x.shape
    L = inputs["w_in"].shape[0]
    return run(inputs, S, L, B).astype(np.float32)
```
